# Optimizing a Trainium2 kernel written in Bass

```python
import math
import jax
import jax.numpy as jnp
from jax import lax
import numpy as np


D_MODEL = 1024
BATCH = 4
SEQ = 8192
DEPTH = 2
DEC_BATCH = 32
DEC_SEQ = 4
PAST_LEN = 16384
PAGE_SIZE = 128

N_A_LAYERS = DEPTH // 2
N_B_LAYERS = DEPTH - N_A_LAYERS
CONV_EXPAND = 2
CONV_CH = CONV_EXPAND * D_MODEL
CONV_WIDTH = 31
HEAD_DIM = 64
HEADS_PER_GROUP = 8
N_GROUPS = 3
WINDOWS = (128, 512, 2048)
DILATIONS = (1, 4, 16)
ATTN_Q_WIDTH = N_GROUPS * HEADS_PER_GROUP * HEAD_DIM
ATTN_OUT_WIDTH = HEADS_PER_GROUP * HEAD_DIM
ROT_DIM = HEAD_DIM // 4
ROPE_THETA = 500000.0
NORM_EPS = 1e-6
BLOCK_Q = 128
ATTN_SCALE = HEAD_DIM ** -0.5
NEG_INF = -1e30

kernel_name = 'conv_yoco_dilated_swa_step'


def _rmsnorm(x, g):
    xf = x.astype(jnp.float32)
    y = xf * lax.rsqrt(jnp.mean(xf * xf, axis=-1, keepdims=True) + NORM_EPS)
    return (y * g.astype(jnp.float32)).astype(x.dtype)


def _layernorm(x, g, b):
    xf = x.astype(jnp.float32)
    xc = xf - jnp.mean(xf, axis=-1, keepdims=True)
    var = jnp.mean(xc * xc, axis=-1, keepdims=True)
    y = xc * lax.rsqrt(var + NORM_EPS) * g.astype(jnp.float32) + b.astype(jnp.float32)
    return y.astype(x.dtype)


def _rope(x, pos):
    half = ROT_DIM // 2
    inv_freq = jnp.exp(-math.log(ROPE_THETA) * jnp.arange(half, dtype=jnp.float32) * (2.0 / ROT_DIM))
    ang = pos.astype(jnp.float32)[:, None] * inv_freq[None, :]
    ang = ang.reshape((pos.shape[0],) + (1,) * (x.ndim - 3) + (half,))
    cos, sin = jnp.cos(ang), jnp.sin(ang)
    xf = x.astype(jnp.float32)
    x1, x2 = xf[..., :half], xf[..., half:ROT_DIM]
    out = jnp.concatenate([x1 * cos - x2 * sin, x2 * cos + x1 * sin, xf[..., ROT_DIM:]], axis=-1)
    return out.astype(x.dtype)


def _conv_layer(x, hist, g_norm, w_in, conv_w, conv_b, ln_g, ln_b, w_out):
    h = _rmsnorm(x, g_norm)
    val, glu_gate, z = jnp.split(h @ w_in, 3, axis=-1)
    v = val * jax.nn.sigmoid(glu_gate)
    full = jnp.concatenate([hist.astype(v.dtype), v], axis=1)
    c = lax.conv_general_dilated(full, conv_w[:, None, :].astype(full.dtype), window_strides=(1,),
                                 padding='VALID', dimension_numbers=('NWC', 'WIO', 'NWC'),
                                 feature_group_count=CONV_CH) + conv_b.astype(full.dtype)
    c = jax.nn.silu(_layernorm(c, ln_g, ln_b))
    y = (c * jax.nn.silu(z)) @ w_out
    return x + y.astype(x.dtype), full[:, -(CONV_WIDTH - 1):]


def _shared_kv(x, pos, kv_norm, w_kv, k_gain):
    bx, length = x.shape[:2]
    h = _rmsnorm(x, kv_norm)
    kv = (h @ w_kv).reshape(bx, length, 2, N_GROUPS, HEADS_PER_GROUP, HEAD_DIM)
    k = _rope(_rmsnorm(kv[:, :, 0], k_gain), pos)
    return k, kv[:, :, 1]


def _query_side(x, pos, g_norm, w_in, q_gain):
    bx, length = x.shape[:2]
    u = _rmsnorm(x, g_norm) @ w_in
    q = u[..., :ATTN_Q_WIDTH].reshape(bx, length, N_GROUPS, HEADS_PER_GROUP, HEAD_DIM)
    q = _rope(_rmsnorm(q, q_gain), pos)
    return q, u[..., ATTN_Q_WIDTH:]


def _band_attention(q, k, v, span):
    assert span <= BLOCK_Q
    lead = q.shape[:-2]
    m_len = q.shape[-2]
    nb = -(-m_len // BLOCK_Q)
    extra = nb * BLOCK_Q - m_len
    pad_q = [(0, 0)] * len(lead) + [(0, extra), (0, 0)]
    pad_kv = [(0, 0)] * len(lead) + [(BLOCK_Q, extra), (0, 0)]
    qb = jnp.pad(q, pad_q).reshape(lead + (nb, BLOCK_Q, HEAD_DIM))
    kb = jnp.pad(k, pad_kv).reshape(lead + (nb + 1, BLOCK_Q, HEAD_DIM))
    vb = jnp.pad(v, pad_kv).reshape(lead + (nb + 1, BLOCK_Q, HEAD_DIM))
    k_band = jnp.concatenate([kb[..., :-1, :, :], kb[..., 1:, :, :]], axis=-2)
    v_band = jnp.concatenate([vb[..., :-1, :, :], vb[..., 1:, :, :]], axis=-2)
    s = jnp.einsum('...nqd,...nkd->...nqk', qb, k_band, preferred_element_type=jnp.float32) * ATTN_SCALE
    qi = jnp.arange(BLOCK_Q)[:, None]
    kj = jnp.arange(2 * BLOCK_Q)[None, :]
    dist = qi + BLOCK_Q - kj
    kpos = jnp.arange(nb)[:, None, None] * BLOCK_Q - BLOCK_Q + kj[None]
    valid = (dist >= 0)[None] & (dist <= span)[None] & (kpos >= 0)
    s = jnp.where(valid, s, NEG_INF)
    mx = jnp.max(s, axis=-1, keepdims=True)
    e = jnp.exp(s - mx)
    den = jnp.sum(e, axis=-1, keepdims=True)
    o = jnp.einsum('...nqk,...nkd->...nqd', e, v_band.astype(jnp.float32)) / den
    lse = (mx + jnp.log(den))[..., 0]
    o = o.reshape(lead + (nb * BLOCK_Q, HEAD_DIM))[..., :m_len, :]
    lse = lse.reshape(lead + (nb * BLOCK_Q,))[..., :m_len]
    return o, lse


def _dilated_group_prompt(q, k, v, window, dilation):
    bx, s_len = q.shape[:2]
    m_len = s_len // dilation

    def to_sub(t):
        return t.reshape(bx, m_len, dilation, HEADS_PER_GROUP, HEAD_DIM).transpose(0, 2, 3, 1, 4)

    o, lse = _band_attention(to_sub(q), to_sub(k), to_sub(v), window // dilation)
    o = o.transpose(0, 3, 1, 2, 4).reshape(bx, s_len, HEADS_PER_GROUP, HEAD_DIM)
    lse = lse.transpose(0, 3, 1, 2).reshape(bx, s_len, HEADS_PER_GROUP)
    return o, lse


def _dilated_group_sample(q, k_cat, v_cat, buf_len, window, dilation):
    t_len = q.shape[1]
    span = window // dilation
    idx = buf_len + jnp.arange(t_len)[:, None] - dilation * jnp.arange(span + 1)[None, :]
    valid = idx >= 0
    idx = jnp.maximum(idx, 0)
    kg = jnp.take(k_cat, idx, axis=1)
    vg = jnp.take(v_cat, idx, axis=1)
    s = jnp.einsum('bthd,btmhd->bhtm', q, kg, preferred_element_type=jnp.float32) * ATTN_SCALE
    s = jnp.where(valid[None, None], s, NEG_INF)
    mx = jnp.max(s, axis=-1, keepdims=True)
    e = jnp.exp(s - mx)
    den = jnp.sum(e, axis=-1, keepdims=True)
    o = jnp.einsum('bhtm,btmhd->bthd', e, vg.astype(jnp.float32)) / den[..., 0].transpose(0, 2, 1)[..., None]
    lse = (mx + jnp.log(den))[..., 0].transpose(0, 2, 1)
    return o, lse


def _merge_groups(outs, lses):
    w = jax.nn.softmax(jnp.stack(lses, axis=0), axis=0)
    return jnp.sum(w[..., None] * jnp.stack(outs, axis=0), axis=0)


def _attn_out(o, z, w_out):
    bx, length = o.shape[:2]
    return (o.reshape(bx, length, ATTN_OUT_WIDTH).astype(z.dtype) * jax.nn.silu(z)) @ w_out


def _window_state_sample(cache, k_new, v_new):
    cat = jnp.concatenate([cache, jnp.stack([k_new, v_new], axis=2).astype(cache.dtype)], axis=1)
    return cat[:, cat.shape[1] - cache.shape[1]:]


def _window_state_prompt(k, v, window):
    rows = min(window, k.shape[1])
    return jnp.stack([k[:, k.shape[1] - rows:], v[:, v.shape[1] - rows:]], axis=2)


def setup_inputs(seed: int = 0) -> dict:
    key = jax.random.key(seed)
    ks = jax.random.split(key, 24)

    def nrm(k, shape, scale):
        return scale * jax.random.normal(k, shape, jnp.float32)

    return {
        'x_prompt': nrm(ks[0], (BATCH, SEQ, D_MODEL), 1.0),
        'x_sample': nrm(ks[1], (DEC_BATCH, DEC_SEQ, D_MODEL), 1.0),
        'cache_conv': nrm(ks[2], (N_A_LAYERS, DEC_BATCH, CONV_WIDTH - 1, CONV_CH), 0.5),
        'cache_kv_w128': nrm(ks[3], (DEC_BATCH, min(WINDOWS[0], PAST_LEN), 2, HEADS_PER_GROUP, HEAD_DIM), 1.0),
        'cache_kv_w512': nrm(ks[4], (DEC_BATCH, min(WINDOWS[1], PAST_LEN), 2, HEADS_PER_GROUP, HEAD_DIM), 1.0),
        'cache_kv_w2048': nrm(ks[5], (DEC_BATCH, min(WINDOWS[2], PAST_LEN), 2, HEADS_PER_GROUP, HEAD_DIM), 1.0),
        'a_norm': 1.0 + nrm(ks[6], (N_A_LAYERS, D_MODEL), 0.02),
        'a_w_in': nrm(ks[7], (N_A_LAYERS, D_MODEL, 3 * CONV_CH), D_MODEL ** -0.5),
        'a_conv_w': nrm(ks[8], (N_A_LAYERS, CONV_WIDTH, CONV_CH), CONV_WIDTH ** -0.5),
        'a_conv_b': nrm(ks[9], (N_A_LAYERS, CONV_CH), 0.01),
        'a_ln_g': 1.0 + nrm(ks[10], (N_A_LAYERS, CONV_CH), 0.02),
        'a_ln_b': nrm(ks[11], (N_A_LAYERS, CONV_CH), 0.01),
        'a_w_out': nrm(ks[12], (N_A_LAYERS, CONV_CH, D_MODEL), CONV_CH ** -0.5),
        'kv_norm': 1.0 + nrm(ks[13], (D_MODEL,), 0.02),
        'w_kv': nrm(ks[14], (D_MODEL, 2 * ATTN_Q_WIDTH), D_MODEL ** -0.5),
        'k_norm': 1.0 + nrm(ks[15], (HEAD_DIM,), 0.02),
        'b_norm': 1.0 + nrm(ks[16], (N_B_LAYERS, D_MODEL), 0.02),
        'b_w_in': nrm(ks[17], (N_B_LAYERS, D_MODEL, ATTN_Q_WIDTH + ATTN_OUT_WIDTH), D_MODEL ** -0.5),
        'q_norm': 1.0 + nrm(ks[18], (N_B_LAYERS, HEAD_DIM), 0.02),
        'b_w_out': nrm(ks[19], (N_B_LAYERS, ATTN_OUT_WIDTH, D_MODEL), ATTN_OUT_WIDTH ** -0.5),
    }


def reference(x_prompt, x_sample, cache_conv, cache_kv_w128, cache_kv_w512, cache_kv_w2048,
              a_norm, a_w_in, a_conv_w, a_conv_b, a_ln_g, a_ln_b, a_w_out,
              kv_norm, w_kv, k_norm, b_norm, b_w_in, q_norm, b_w_out):
    s_len = x_prompt.shape[1]
    t_len = x_sample.shape[1]
    pos_p = jnp.arange(s_len, dtype=jnp.int32)
    pos_s = PAST_LEN + jnp.arange(t_len, dtype=jnp.int32)
    caches = (cache_kv_w128, cache_kv_w512, cache_kv_w2048)
    xp, xs = x_prompt, x_sample
    conv_p, conv_s = [], []
    kp = vp = ks = vs = None
    k_cat, v_cat = [], []
    for layer in range(DEPTH):
        if layer < N_A_LAYERS:
            a = layer
            hist0 = jnp.zeros((xp.shape[0], CONV_WIDTH - 1, CONV_CH), xp.dtype)
            xp, hp = _conv_layer(xp, hist0, a_norm[a], a_w_in[a], a_conv_w[a], a_conv_b[a],
                                 a_ln_g[a], a_ln_b[a], a_w_out[a])
            xs, hs = _conv_layer(xs, cache_conv[a], a_norm[a], a_w_in[a], a_conv_w[a], a_conv_b[a],
                                 a_ln_g[a], a_ln_b[a], a_w_out[a])
            conv_p.append(hp)
            conv_s.append(hs)
        else:
            if layer == N_A_LAYERS:
                kp, vp = _shared_kv(xp, pos_p, kv_norm, w_kv, k_norm)
                ks, vs = _shared_kv(xs, pos_s, kv_norm, w_kv, k_norm)
                for g in range(N_GROUPS):
                    k_cat.append(jnp.concatenate([caches[g][:, :, 0].astype(ks.dtype), ks[:, :, g]], axis=1))
                    v_cat.append(jnp.concatenate([caches[g][:, :, 1].astype(vs.dtype), vs[:, :, g]], axis=1))
            b = layer - N_A_LAYERS
            qp, zp = _query_side(xp, pos_p, b_norm[b], b_w_in[b], q_norm[b])
            qs, zs = _query_side(xs, pos_s, b_norm[b], b_w_in[b], q_norm[b])
            outs_p, lses_p, outs_s, lses_s = [], [], [], []
            for g in range(N_GROUPS):
                o, l = _dilated_group_prompt(qp[:, :, g], kp[:, :, g], vp[:, :, g], WINDOWS[g], DILATIONS[g])
                outs_p.append(o)
                lses_p.append(l)
                o, l = _dilated_group_sample(qs[:, :, g], k_cat[g], v_cat[g], caches[g].shape[1],
                                             WINDOWS[g], DILATIONS[g])
                outs_s.append(o)
                lses_s.append(l)
            xp = xp + _attn_out(_merge_groups(outs_p, lses_p), zp, b_w_out[b]).astype(xp.dtype)
            xs = xs + _attn_out(_merge_groups(outs_s, lses_s), zs, b_w_out[b]).astype(xs.dtype)
    new_conv_prompt = jnp.stack(conv_p, axis=0)
    new_conv_sample = jnp.stack(conv_s, axis=0)
    new_kv_w128_prompt = _window_state_prompt(kp[:, :, 0], vp[:, :, 0], WINDOWS[0])
    new_kv_w128_sample = _window_state_sample(cache_kv_w128, ks[:, :, 0], vs[:, :, 0])
    new_kv_w512_prompt = _window_state_prompt(kp[:, :, 1], vp[:, :, 1], WINDOWS[1])
    new_kv_w512_sample = _window_state_sample(cache_kv_w512, ks[:, :, 1], vs[:, :, 1])
    new_kv_w2048_prompt = _window_state_prompt(kp[:, :, 2], vp[:, :, 2], WINDOWS[2])
    new_kv_w2048_sample = _window_state_sample(cache_kv_w2048, ks[:, :, 2], vs[:, :, 2])
    return (xp, xs, new_conv_prompt, new_conv_sample, new_kv_w128_prompt, new_kv_w128_sample,
            new_kv_w512_prompt, new_kv_w512_sample, new_kv_w2048_prompt, new_kv_w2048_sample)
```

```python
import math
import os
KSKEW = os.environ.get('KSKEW', 'ACD')
KE_PIPE = os.environ.get('KE_PIPE', '1') == '1'
from contextlib import ExitStack
import numpy as np
import concourse.bass as bass
import concourse.mybir as mybir
from concourse.bass_utils import run_bass_kernel_spmd

F32 = mybir.dt.float32
BF16 = mybir.dt.bfloat16
AF = mybir.ActivationFunctionType
ALU = mybir.AluOpType
AX = mybir.AxisListType

D = 1024
CH = 2048
NCH = 16
CW = 31
PRE = 32
HALO = 2048
MAIN = 4096
R = HALO + MAIN
RP = PRE + R
TA = 256
EPS = 1e-6
SCALE = 64 ** -0.5
WINS = (128, 512, 2048)
DILS = (1, 4, 16)
PAST = 16384


class Tok:
    __slots__ = ("name", "w", "r")

    def __init__(self, name):
        self.name = name
        self.w = {}
        self.r = {}


class KB:
    def __init__(self, nc):
        self.nc = nc
        self.engs = {"pe": nc.tensor, "act": nc.scalar, "dve": nc.vector, "pool": nc.gpsimd, "sp": nc.sync}
        self.sems = {}
        self.total = {}
        self.isdma = {}
        self.seen = {e: {} for e in self.engs}
        self.cur = {}
        self.outh = {}
        self.nops = 0
        for e in ("pe", "act", "dve", "pool"):
            self.new_eng_sem(e, 0)

    def new_eng_sem(self, e, gen):
        key = "%s%d" % (e, gen)
        self.sems[key] = self.nc.alloc_semaphore(name="s_" + key)
        self.total[key] = 0
        self.isdma[key] = False
        self.cur[e] = key

    def dsem(self, name):
        key = "d_" + name
        if key not in self.sems:
            self.sems[key] = self.nc.alloc_semaphore(name=key)
            self.total[key] = 0
            self.isdma[key] = True
        return key

    def _wait(self, eng, deps, skip_self=False):
        for key, cnt in deps.items():
            if self.isdma[key]:
                cnt = self.total[key]
            elif eng == "pe" and key.startswith("pe"):
                continue
            elif skip_self and key.startswith(eng):
                continue
            if self.seen[eng].get(key, 0) >= cnt:
                continue
            self.engs[eng].wait_ge(self.sems[key], cnt)
            self.seen[eng][key] = cnt

    @staticmethod
    def _merge(d, o):
        for k, v in o.items():
            if d.get(k, 0) < v:
                d[k] = v

    def _deps(self, reads, writes):
        deps = {}
        for t in reads:
            self._merge(deps, t.w)
        for t in writes:
            self._merge(deps, t.w)
            self._merge(deps, t.r)
        return deps

    def op(self, eng, fn, reads=(), writes=(), skip_self=False):
        self._wait(eng, self._deps(reads, writes), skip_self)
        ins = fn(self.engs[eng])
        key = self.cur[eng]
        self.total[key] += 1
        ins.then_inc(self.sems[key], 1)
        h = {key: self.total[key]}
        for t in reads:
            self._merge(t.r, h)
        for t in writes:
            t.w = dict(h)
            t.r = {}
        self.nops += 1
        return h

    def dma(self, q, sem, out, in_, reads=(), writes=(), is_out=False, slow=False):
        self._wait(q, self._deps(reads, writes))
        key = self.dsem(sem)
        ins = self.engs[q].dma_start(out=out, in_=in_, allow_slow_non_contiguous=True) if slow else self.engs[q].dma_start(out=out, in_=in_)
        self.total[key] += 16
        ins.then_inc(self.sems[key], 16)
        h = {key: self.total[key]}
        for t in reads:
            self._merge(t.r, h)
        for t in writes:
            t.w = dict(h)
            t.r = {}
        if is_out:
            self._merge(self.outh, h)
        self.nops += 1
        return h

    def barrier(self):
        alls = {k: v for k, v in self.total.items() if v > 0}
        for e in self.engs:
            self._wait(e, {k: v for k, v in alls.items() if not (e != "pe" and False)})

    def finish(self):
        alls = {k: v for k, v in self.total.items() if v > 0 and self.isdma[k]}
        self._wait("sp", alls)


def sap(t, p0, np_, off, dims):
    ps = 1
    for s in t.shape[1:]:
        ps *= s
    return bass.AP(t, p0 * ps + off, [[ps, np_]] + [list(d) for d in dims])


def dap(t, off, dims):
    return bass.AP(t, off, [list(d) for d in dims])


def build(debug=False):
    nc = bass.Bass("TRN2", target_bir_lowering=False)
    kb = KB(nc)
    din = lambda n, s: nc.dram_tensor(n, list(s), F32, kind="ExternalInput")
    dout = lambda n, s: nc.dram_tensor(n, list(s), F32, kind="ExternalOutput")
    dscr = lambda n, s, dt: nc.dram_tensor(n, list(s), dt, kind=("ExternalOutput" if debug else "Internal"))

    xr = din("xr", [RP, D]); xs = din("xs", [16, D]); cconv = din("cconv", [4, 30, CH])
    ck = [din("ck%d" % g, [4, WINS[g], 2, 8, 64]) for g in range(3)]
    a_norm = din("a_norm", [1, D]); a_w_in = din("a_w_in", [1, D, 3 * CH]); a_conv_w = din("a_conv_w", [1, CW, CH])
    a_conv_b = din("a_conv_b", [1, CH]); a_ln_g = din("a_ln_g", [1, CH]); a_ln_b = din("a_ln_b", [1, CH])
    a_w_out = din("a_w_out", [1, CH, D]); kv_norm = din("kv_norm", [D]); w_kv = din("w_kv", [D, 3072])
    k_norm = din("k_norm", [64]); b_norm = din("b_norm", [1, D]); b_w_in = din("b_w_in", [1, D, 2048])
    q_norm = din("q_norm", [1, 64]); b_w_out = din("b_w_out", [1, 512, D])
    csp = din("csp", [R, 16]); css = din("css", [16, 16])
    identf_d = din("identf", [128, 128]); maskn_d = din("maskn", [128, 256]); maskh_d = din("maskh", [128, 256])
    smask_d = din("smask", [128, 4]); nmask_d = din("nmask", [16, 48]); selr_d = din("selr", [16, 16 * 128])
    selc_d = din("selc", [128, 16 * 16])

    yp = dout("yp", [MAIN, D]); ys = dout("ys", [16, D]); ncp = dout("ncp", [32, CH]); ncs = dout("ncs", [4, 30, CH])
    nkp = [dout("nk%dp" % g, [WINS[g], 2, 512]) for g in range(3)]
    nks = [dout("nk%ds" % g, [4, WINS[g], 2, 512]) for g in range(3)]

    vT_d = dscr("vT_d", [R // TA, 128, NCH * TA], BF16); vpre_d = dscr("vpre_d", [128, NCH * PRE], BF16)
    szT_d = dscr("szT_d", [R // TA, 128, NCH * TA], BF16); cT_d = dscr("cT_d", [R // 256, 128, NCH * 256], F32)
    x1_d = dscr("x1_d", [R, D], F32); kv_d = dscr("kv_d", [R, 3, 2, 512], BF16); q_d = dscr("q_d", [MAIN, 3, 512], BF16)
    szg_d = dscr("szg_d", [MAIN, 512], BF16)
    dtok = {}

    def DT(name, i):
        k = (name, i)
        if k not in dtok:
            dtok[k] = Tok("%s_%s" % (name, i))
        return dtok[k]

    uniq = [0]

    def sbt(es, n, s, d):
        uniq[0] += 1
        return es.enter_context(nc.sbuf_tensor("sb%d_%s" % (uniq[0], n), list(s), d))

    pst = lambda es, n, s, d: es.enter_context(nc.psum_tensor("ps_" + n, list(s), d))

    with ExitStack() as top:
        PF = [pst(top, "pf%d" % i, [128, 512], F32) for i in range(6)]
        PB = [pst(top, "pb%d" % i, [128, 1024], BF16) for i in range(2)]
        tPF = [Tok("pf%d" % i) for i in range(6)]
        tPB = [Tok("pb%d" % i) for i in range(2)]
        identf = sbt(top, "identf", [128, 128], F32); identb = sbt(top, "identb", [128, 128], BF16)
        onesf = sbt(top, "onesf", [128, 128], F32); onesb = sbt(top, "onesb", [128, 64], BF16)
        tconst = Tok("const")
        vTs = sbt(top, "vTs", [128, NCH, 16], BF16); szs = sbt(top, "szs", [128, NCH, 16], BF16)
        cTs = sbt(top, "cTs", [128, NCH, 16], F32)
        tvTs = Tok("vTs"); tszs = Tok("szs"); tcTs = Tok("cTs")
        x1s = sbt(top, "x1s", [16, 1, D], F32); tx1s = Tok("x1s")
        kvs = sbt(top, "kvs", [16, 3, 2, 512], F32); tkvs = Tok("kvs")
        qs = sbt(top, "qs", [16, 3, 512], F32); tqs = Tok("qs")
        szgs = sbt(top, "szgs", [16, 512], F32); tszgs = Tok("szgs")
        ss = sbt(top, "ss", [128, 8], F32); tss = Tok("ss")
        cw = sbt(top, "cw", [128, NCH, 34], F32); tcw = Tok("cw")

        kb.dma("sp", "const", identf[:], identf_d.ap(), writes=[tconst])
        kb.op("dve", lambda e: e.tensor_copy(out=identb[:], in_=identf[:]), reads=[tconst], writes=[tconst])
        kb.op("dve", lambda e: e.memset(onesf[:], 1.0), writes=[tconst])
        kb.op("dve", lambda e: e.memset(onesb[:], 1.0), writes=[tconst])

        cnt = {"cv": 0, "pb": 0}

        def evac_eng():
            cnt["cv"] += 1
            return "act" if cnt["cv"] % 2 else "dve"

        def next_pb():
            cnt["pb"] += 1
            return PB[cnt["pb"] % 2], tPB[cnt["pb"] % 2]

        def copy_op(eng, out, in_, reads, writes):
            if eng == "act":
                return kb.op("act", lambda e: e.activation(out=out, in_=in_, func=AF.Copy), reads=reads, writes=writes)
            return kb.op(eng, lambda e: e.tensor_copy(out=out, in_=in_), reads=reads, writes=writes)

        def load_weights(specs):
            with ExitStack() as es2:
                stg = [sbt(es2, "stg%d" % i, [128, 2048], F32) for i in range(3)]
                tstg = [Tok("stg%d" % i) for i in range(3)]
                gts = []
                for wi, (W, tW, src, gain) in enumerate(specs):
                    kparts, nk, cols = W.shape[0], W.shape[1], W.shape[2]
                    g = None
                    if gain is not None:
                        g = sbt(es2, "wg%d" % wi, [kparts, nk], F32)
                        kb.dma("sp", "wg", sap(g, 0, kparts, 0, [[1, nk], [1, 1]]),
                               dap(gain[0], gain[1], [[1, kparts], [kparts, nk], [1, 1]]), writes=[tW], slow=True)
                    gts.append(g)
                i = 0
                for wi, (W, tW, src, gain) in enumerate(specs):
                    kparts, nk, cols = W.shape[0], W.shape[1], W.shape[2]
                    g = gts[wi]
                    for k in range(nk):
                        for c0 in range(0, cols, 2048):
                            cwid = min(2048, cols - c0)
                            b = i % 3
                            kb.dma("sp" if i % 2 == 0 else "act", "stg%d" % b, stg[b][0:kparts, 0:cwid],
                                   src[k * kparts:(k + 1) * kparts, c0:c0 + cwid], writes=[tstg[b]])
                            eng = ("act", "dve", "act", "dve", "pool")[i % 5]
                            o = W[:, k, c0:c0 + cwid]
                            in_ = stg[b][0:kparts, 0:cwid]
                            if g is not None:
                                sc = g[:, k:k + 1]
                                if eng == "act":
                                    kb.op("act", lambda e: e.activation(out=o, in_=in_, func=AF.Copy, scale=sc),
                                          reads=[tstg[b], tW], writes=[tW])
                                else:
                                    kb.op(eng, lambda e: e.tensor_scalar(out=o, in0=in_, scalar1=sc, scalar2=None, op0=ALU.mult),
                                          reads=[tstg[b], tW], writes=[tW])
                            else:
                                copy_op(eng, o, in_, [tstg[b], tW], [tW])
                            i += 1
                kb.barrier()

        def rms_T(xt, txt, nt, hT, thT, xn, txn, junk, tjunk):
            nsub = (nt + 127) // 128
            for s in range(nsub):
                np_ = min(128, nt - s * 128)
                kb.op("act", lambda e: e.activation(out=junk[0:np_, :], in_=xt[0:np_, s, :], func=AF.Square,
                                                    accum_out=ss[0:np_, s:s + 1]), reads=[txt], writes=[tjunk, tss])
            npm = min(128, nt)
            kb.op("act", lambda e: e.activation(out=ss[0:npm, 4:4 + nsub], in_=ss[0:npm, 0:nsub], func=AF.Sqrt,
                                                scale=1.0 / D, bias=EPS), reads=[tss], writes=[tss])
            kb.op("dve", lambda e: e.reciprocal(out=ss[0:npm, 0:nsub], in_=ss[0:npm, 4:4 + nsub]), reads=[tss], writes=[tss])
            for s in range(nsub):
                np_ = min(128, nt - s * 128)
                eng = "dve" if s % 2 == 0 else "pool"
                kb.op(eng, lambda e: e.tensor_scalar(out=xn[0:np_, s, :], in0=xt[0:np_, s, :], scalar1=ss[0:np_, s:s + 1],
                                                     scalar2=None, op0=ALU.mult), reads=[txt, tss], writes=[txn])
            for k in range(8):
                pb, tpb = next_pb()
                for s in range(nsub):
                    np_ = min(128, nt - s * 128)
                    kb.op("pe", lambda e: e.transpose(out=pb[:, s * 128:s * 128 + np_], in_=xn[0:np_, s, k * 128:(k + 1) * 128],
                                                      identity=identb[0:np_, 0:np_]), reads=[txn, tconst], writes=[tpb])
                copy_op(evac_eng(), hT[:, k, 0:nt], pb[:, 0:nt], [tpb], [thT])

        with ExitStack() as es:
            Win = sbt(es, "Win", [128, 8, 3 * CH], BF16); tWin = Tok("Win")
            load_weights([(Win, tWin, a_w_in.ap()[0], (a_norm, 0))])
            if debug == "W":
                kb.finish()
                return nc
            XT = [sbt(es, "xtA%d" % i, [128, TA // 128, D], F32) for i in range(2)]; tXT = [Tok("xtA%d" % i) for i in range(2)]
            xn = sbt(es, "xnA", [128, TA // 128, D], BF16); txn = Tok("xnA")
            junk = sbt(es, "junkA", [128, D], BF16); tjunk = Tok("junkA")
            HTa = [sbt(es, "hTA%d" % i, [128, 8, TA], BF16) for i in range(2)]; tHTa = [Tok("hTA%d" % i) for i in range(2)]
            VA = [sbt(es, "vallA%d" % i, [128, NCH, TA], BF16) for i in range(2)]; tVA = [Tok("vallA%d" % i) for i in range(2)]
            SZ = [sbt(es, "szallA%d" % i, [128, NCH, TA], BF16) for i in range(2)]; tSZ = [Tok("szallA%d" % i) for i in range(2)]
            sg = [sbt(es, "sgA%d" % i, [128, 512], F32) for i in range(2)]; tsg = [Tok("sgA%d" % i) for i in range(2)]
            vtm = sbt(es, "vtm", [32, CH], F32); tvtm = Tok("vtm")

            tiles = [("pre", 0, PRE)] + [("main", PRE + t * TA, TA) for t in range(R // TA)] + [("smp", 0, 16)]
            XNa = [xn, sbt(es, "xnA2", [128, TA // 128, D], BF16)]; tXNa = [txn, Tok("xnA2")]
            SSa = [sbt(es, "ssA%d" % i, [128, 8], F32) for i in range(2)]; tSSa = [Tok("ssA%d" % i) for i in range(2)]

            def A_load(it):
                kind, r0, nt = tiles[it]
                xt = XT[it % 2]; txt = tXT[it % 2]
                nsub = (nt + 127) // 128
                if kind == "smp":
                    kb.dma("sp", "xtA%d" % (it % 2), xt[0:nt, 0, :], xs.ap(), writes=[txt])
                elif nt < 128:
                    kb.dma("sp", "xtA%d" % (it % 2), xt[0:nt, 0, :], xr.ap()[r0:r0 + nt, :], writes=[txt])
                else:
                    kb.dma("sp", "xtA%d" % (it % 2), xt[:, 0:nsub, :],
                           dap(xr, r0 * D, [[D, 128], [128 * D, nsub], [1, D]]), writes=[txt])

            def A_norm(it):
                kind, r0, nt = tiles[it]
                b = it % 2
                xt = XT[b]; txt = tXT[b]; sa = SSa[b]; tsa = tSSa[b]
                nsub = (nt + 127) // 128
                npm = min(128, nt)
                for s_ in range(nsub):
                    np_ = min(128, nt - s_ * 128)
                    kb.op("act", lambda e: e.activation(out=junk[0:np_, :], in_=xt[0:np_, s_, :], func=AF.Square,
                                                        accum_out=sa[0:np_, s_:s_ + 1]), reads=[txt], writes=[tjunk, tsa])
                kb.op("act", lambda e: e.activation(out=sa[0:npm, 4:4 + nsub], in_=sa[0:npm, 0:nsub], func=AF.Sqrt,
                                                    scale=1.0 / D, bias=EPS), reads=[tsa], writes=[tsa])
                kb.op("dve", lambda e: e.reciprocal(out=sa[0:npm, 0:nsub], in_=sa[0:npm, 4:4 + nsub]), reads=[tsa], writes=[tsa])
                for s_ in range(nsub):
                    np_ = min(128, nt - s_ * 128)
                    eng = "dve" if s_ % 2 == 0 else "pool"
                    kb.op(eng, lambda e: e.tensor_scalar(out=XNa[b][0:np_, s_, :], in0=xt[0:np_, s_, :], scalar1=sa[0:np_, s_:s_ + 1],
                                                         scalar2=None, op0=ALU.mult), reads=[txt, tsa], writes=[tXNa[b]])

            def A_TT(it):
                kind, r0, nt = tiles[it]
                b = it % 2
                nsub = (nt + 127) // 128
                for k in range(8):
                    pb, tpb = next_pb()
                    for s_ in range(nsub):
                        np_ = min(128, nt - s_ * 128)
                        kb.op("pe", lambda e: e.transpose(out=pb[:, s_ * 128:s_ * 128 + np_], in_=XNa[b][0:np_, s_, k * 128:(k + 1) * 128],
                                                          identity=identb[0:np_, 0:np_]), reads=[tXNa[b], tconst], writes=[tpb])
                    copy_op("act", HTa[b][:, k, 0:nt], pb[:, 0:nt], [tpb], [tHTa[b]])

            def A_p2(it, part):
                kind, r0, nt = tiles[it]
                hT = HTa[it % 2]; thT = tHTa[it % 2]
                nsub = (nt + 127) // 128
                va = VA[it % 2]; tva = tVA[it % 2]; sz = SZ[it % 2]; tsz = tSZ[it % 2]
                if kind == "smp":
                    va_o = lambda c: vTs[:, c, 0:nt]
                    sz_o = lambda c: szs[:, c, 0:nt]
                    tva = tvTs; tsz = tszs
                else:
                    va_o = lambda c: va[:, c, 0:nt]
                    sz_o = lambda c: sz[:, c, 0:nt]
                for c in (range(NCH) if part == "a" else ()):
                    pv = PF[(c % 2) * 2]; tpv = tPF[(c % 2) * 2]
                    pg = PF[(c % 2) * 2 + 1]; tpg = tPF[(c % 2) * 2 + 1]
                    for k in range(8):
                        kb.op("pe", lambda e: e.matmul(pv[:, 0:nt], lhsT=Win[:, k, c * 128:(c + 1) * 128], rhs=hT[:, k, 0:nt],
                                                       start=(k == 0), stop=(k == 7)), reads=[tWin, thT], writes=[tpv])
                    for k in range(8):
                        kb.op("pe", lambda e: e.matmul(pg[:, 0:nt], lhsT=Win[:, k, CH + c * 128:CH + (c + 1) * 128],
                                                       rhs=hT[:, k, 0:nt], start=(k == 0), stop=(k == 7)),
                              reads=[tWin, thT], writes=[tpg])
                    s_ = sg[c % 2]; ts_ = tsg[c % 2]
                    kb.op("act", lambda e: e.activation(out=s_[:, 0:nt], in_=pg[:, 0:nt], func=AF.Sigmoid),
                          reads=[tpg], writes=[ts_])
                    kb.op("dve", lambda e: e.tensor_tensor(out=va_o(c), in0=pv[:, 0:nt], in1=s_[:, 0:nt], op=ALU.mult),
                          reads=[tpv, ts_], writes=[tva], skip_self=True)
                if part == "a":
                    return
                if kind != "pre":
                    for c in range(NCH):
                        pz = PF[4 + c % 2]; tpz = tPF[4 + c % 2]
                        for k in range(8):
                            kb.op("pe", lambda e: e.matmul(pz[:, 0:nt], lhsT=Win[:, k, 2 * CH + c * 128:2 * CH + (c + 1) * 128],
                                                           rhs=hT[:, k, 0:nt], start=(k == 0), stop=(k == 7)),
                                  reads=[tWin, thT], writes=[tpz])
                        kb.op("act", lambda e: e.activation(out=sz_o(c), in_=pz[:, 0:nt], func=AF.Silu),
                              reads=[tpz], writes=[tsz], skip_self=True)
                special = (kind == "smp") or (kind == "main" and r0 + nt == RP)
                if special:
                    ntm = 16 if kind == "smp" else 32
                    t0 = 0 if kind == "smp" else nt - 32
                    for n in range(4):
                        pv = PF[0]; pg = PF[1]
                        for k in range(8):
                            kb.op("pe", lambda e: e.matmul(pv[0:ntm, :], lhsT=hT[:, k, t0:t0 + ntm],
                                                           rhs=Win[:, k, n * 512:(n + 1) * 512], start=(k == 0), stop=(k == 7)),
                                  reads=[tWin, thT], writes=[tPF[0]])
                        for k in range(8):
                            kb.op("pe", lambda e: e.matmul(pg[0:ntm, :], lhsT=hT[:, k, t0:t0 + ntm],
                                                           rhs=Win[:, k, CH + n * 512:CH + (n + 1) * 512], start=(k == 0), stop=(k == 7)),
                                  reads=[tWin, thT], writes=[tPF[1]])
                        kb.op("act", lambda e: e.activation(out=sg[0][0:ntm, :], in_=pg[0:ntm, :], func=AF.Sigmoid),
                              reads=[tPF[1]], writes=[tsg[0]])
                        kb.op("dve", lambda e: e.tensor_tensor(out=vtm[0:ntm, n * 512:(n + 1) * 512], in0=pv[0:ntm, :],
                                                               in1=sg[0][0:ntm, :], op=ALU.mult),
                              reads=[tPF[0], tsg[0]], writes=[tvtm])
                    if kind == "smp":
                        for s in range(4):
                            kb.dma("pool", "vtm", ncs.ap()[s, 26:30, :], vtm[4 * s:4 * s + 4, :], reads=[tvtm], is_out=True)
                    else:
                        kb.dma("pool", "vtm", ncp.ap(), vtm[0:32, :], reads=[tvtm], is_out=True)
                if kind == "pre":
                    kb.dma("pool", "vallA%d" % (it % 2), dap(vpre_d, 0, [[NCH * PRE, 128], [PRE, NCH], [1, PRE]]),
                           va[:, :, 0:PRE], reads=[tva], writes=[DT("vT", -1)])
                elif kind == "main":
                    ti = (r0 - PRE) // TA
                    kb.dma("pool", "vallA%d" % (it % 2), vT_d.ap()[ti], sap(va, 0, 128, 0, [[1, NCH * TA]]),
                           reads=[tva], writes=[DT("vT", ti)])
                    kb.dma("pool", "szallA%d" % (it % 2), szT_d.ap()[ti], sap(sz, 0, 128, 0, [[1, NCH * TA]]),
                           reads=[tsz], writes=[DT("szT", ti)])

            nA = len(tiles)
            A_load(0); A_load(1)
            A_norm(0); A_TT(0)
            for it in range(nA):
                if it + 1 < nA:
                    A_norm(it + 1)
                A_p2(it, "a")
                if it + 1 < nA:
                    A_TT(it + 1)
                A_p2(it, "b")
                if it + 2 < nA:
                    A_load(it + 2)
            for s in range(4):
                kb.dma("sp", "d2d", ncs.ap()[s, 0:26, :], cconv.ap()[s, 4:30, :], is_out=True)
            kb.barrier()
        if debug == "A":
            kb.finish()
            return nc

        with ExitStack() as es:
            Dg = sbt(es, "Dg", [128, NCH, CW, 128], BF16); tDg = Tok("Dg")
            tDgE = {"dve": Tok("DgD"), "pool": Tok("DgP"), "act": Tok("DgA")}
            with ExitStack() as es_p:
                prm = sbt(es_p, "prm", [34, CH], F32); tprm = Tok("prm")
                kb.dma("sp", "prm", prm[0:31, :], a_conv_w.ap()[0], writes=[tprm])
                kb.dma("sp", "prm", prm[31:32, :], a_conv_b.ap(), writes=[tprm])
                kb.dma("sp", "prm", prm[32:33, :], a_ln_g.ap(), writes=[tprm])
                kb.dma("sp", "prm", prm[33:34, :], a_ln_b.ap(), writes=[tprm])
                for c in range(NCH):
                    pf = PF[c % 2]; tpf = tPF[c % 2]
                    kb.op("pe", lambda e: e.transpose(out=pf[:, 0:34], in_=prm[0:34, c * 128:(c + 1) * 128], identity=identf[0:34, 0:34]),
                          reads=[tprm, tconst], writes=[tpf])
                    copy_op(evac_eng(), cw[:, c, :], pf[:, 0:34], [tpf, tcw], [tcw])
                for c in range(NCH):
                    eng = ("act", "dve", "act", "dve", "act", "dve", "act", "pool")[c % 8]
                    for j in range(CW):
                        if eng == "act":
                            kb.op("act", lambda e: e.activation(out=Dg[:, c, j, :], in_=identb[:], func=AF.Copy, scale=cw[:, c, j:j + 1]),
                                  reads=[tconst, tcw], writes=[tDgE[eng]], skip_self=not (j == 0 and c < 8))
                        else:
                            kb.op(eng, lambda e: e.tensor_scalar(out=Dg[:, c, j, :], in0=identb[:], scalar1=cw[:, c, j:j + 1],
                                                                 scalar2=None, op0=ALU.mult), reads=[tconst, tcw], writes=[tDgE[eng]],
                                  skip_self=not (j == 0 and c < 8))
                kb.barrier()
            TB = 256
            with ExitStack() as es_m:
                VIN = [sbt(es_m, "vin%d" % i, [128, NCH, 30 + TB], BF16) for i in range(2)]; tVIN = [Tok("vin%d" % i) for i in range(2)]
                VST = [sbt(es_m, "vst%d" % i, [128, NCH * TB], BF16) for i in range(2)]; tVST = [Tok("vst%d" % i) for i in range(2)]
                CTO = [sbt(es_m, "cto%d" % i, [128, 8, TB], F32) for i in range(2)]; tCTO = [Tok("cto%d" % i) for i in range(2)]
                for t in range(R // TB):
                    vin = VIN[t % 2]; tvin = tVIN[t % 2]; vst = VST[t % 2]; tvst = tVST[t % 2]
                    kb.dma("sp", "vst%d" % (t % 2), vst[:, :], vT_d.ap()[t], reads=[DT("vT", t)], writes=[tvst])
                    if t == 0:
                        kb.dma("sp", "vin0", vin[:, :, 0:30], dap(vpre_d, 2, [[NCH * PRE, 128], [PRE, NCH], [1, 30]]),
                               reads=[DT("vT", -1)], writes=[tvin])
                    else:
                        kb.op("dve", lambda e: e.tensor_copy(out=vin[:, :, 0:30], in_=VIN[(t - 1) % 2][:, :, TB:TB + 30]),
                              reads=[tVIN[(t - 1) % 2]], writes=[tvin])
                    kb.op("dve", lambda e: e.tensor_copy(out=vin[:, :, 30:30 + TB], in_=sap(vst, 0, 128, 0, [[TB, NCH], [1, TB]])),
                          reads=[tvst, tvin], writes=[tvin])
                    for c in range(NCH):
                        hf = c // 8
                        cto = CTO[hf]; tcto = tCTO[hf]
                        pf = PF[c % 6]; tpf = tPF[c % 6]
                        for j in range(CW):
                            kb.op("pe", lambda e: e.matmul(pf[:, 0:TB], lhsT=Dg[:, c, j, :], rhs=vin[:, c, j:j + TB],
                                                           start=(j == 0), stop=(j == CW - 1)), reads=[tDgE["dve"], tDgE["pool"], tDgE["act"], tvin], writes=[tpf])
                        kb.op("act", lambda e: e.activation(out=cto[:, c % 8, :], in_=pf[:, 0:TB], func=AF.Identity,
                                                            bias=cw[:, c, 31:32], scale=1.0), reads=[tpf, tcw, tcto], writes=[tcto])
                        if c % 8 == 7:
                            kb.dma("pool", "cto%d" % hf, dap(cT_d, t * 128 * NCH * 256 + hf * 8 * 256, [[NCH * 256, 128], [1, 8 * 256]]),
                                   sap(cto, 0, 128, 0, [[1, 8 * TB]]), reads=[tcto], writes=[DT("cT", (t, hf))])
                kb.barrier()
            prm = sbt(es, "hist", [30, CH], F32); tprm = Tok("hist")
            vins = sbt(es, "vins", [128, NCH, 4, 34], BF16); tvins = Tok("vins")
            for s in range(4):
                kb.dma("sp", "hist", prm[0:30, :], cconv.ap()[s], writes=[tprm])
                pf = PF[s % 2]; tpf = tPF[s % 2]
                for c in range(NCH):
                    kb.op("pe", lambda e: e.transpose(out=pf[:, c * 30:(c + 1) * 30], in_=prm[0:30, c * 128:(c + 1) * 128],
                                                      identity=identf[0:30, 0:30]), reads=[tprm, tconst], writes=[tpf])
                kb.op("dve", lambda e: e.tensor_copy(out=sap(vins, 0, 128, s * 34, [[4 * 34, NCH], [1, 30]]),
                                                     in_=sap(pf, 0, 128, 0, [[30, NCH], [1, 30]])), reads=[tpf, tvins], writes=[tvins])
            kb.op("dve", lambda e: e.tensor_copy(out=sap(vins, 0, 128, 30, [[4 * 34, NCH], [34, 4], [1, 4]]),
                                                 in_=sap(vTs, 0, 128, 0, [[16, NCH], [4, 4], [1, 4]])), reads=[tvTs, tvins], writes=[tvins])
            for c in range(NCH):
                pf = PF[c % 6]; tpf = tPF[c % 6]
                for j in range(CW):
                    kb.op("pe", lambda e: e.matmul(pf[:, 0:16], lhsT=Dg[:, c, j, :],
                                                   rhs=sap(vins, 0, 128, c * 4 * 34 + j, [[34, 4], [1, 4]]),
                                                   start=(j == 0), stop=(j == CW - 1)), reads=[tDgE["dve"], tDgE["pool"], tDgE["act"], tvins], writes=[tpf])
                kb.op("act", lambda e: e.activation(out=cTs[:, c, :], in_=pf[:, 0:16], func=AF.Identity,
                                                    bias=cw[:, c, 31:32], scale=1.0), reads=[tpf, tcw, tcTs], writes=[tcTs])
            kb.barrier()
        if debug == "B":
            kb.finish()
            return nc

        TC = 256
        with ExitStack() as es:
            Wout = sbt(es, "Wout", [128, NCH, D], BF16); tWout = Tok("Wout")
            load_weights([(Wout, tWout, a_w_out.ap()[0], None)])
            NS = TC // 128
            CT = [sbt(es, "ctc%d" % i, [128, NCH, TC], F32) for i in range(2)]; tCT = [Tok("ctc%d" % i) for i in range(2)]
            SZc = [sbt(es, "szc%d" % i, [128, NCH, TC], BF16) for i in range(2)]; tSZc = [Tok("szc%d" % i) for i in range(2)]
            XTc = [sbt(es, "xtc%d" % i, [128, NS, D], F32) for i in range(3)]; tXTc = [Tok("xtc%d" % i) for i in range(3)]
            sq = [sbt(es, "sqc%d" % i, [128, TC], F32) for i in range(2)]; tsq = [Tok("sqc%d" % i) for i in range(2)]
            STc = [sbt(es, "stc%d" % i, [128, 4, TC], F32) for i in range(2)]; tSTc = [Tok("stc%d" % i) for i in range(2)]
            nrm = [sbt(es, "nrm%d" % i, [128, TC], F32) for i in range(2)]; tnrm = [Tok("nrm%d" % i) for i in range(2)]
            act_ = [sbt(es, "actc%d" % i, [128, TC], BF16) for i in range(2)]; tact = [Tok("actc%d" % i) for i in range(2)]
            Uc = [sbt(es, "uc%d" % i, [128, NCH, TC], BF16) for i in range(2)]; tUc = [(Tok("ucD%d" % i), Tok("ucP%d" % i)) for i in range(2)]
            tiles = [("main", t * TC, TC) for t in range(R // TC)] + [("smp", 0, 16)]
            def C_views(it):
                kind, i0, nt = tiles[it]
                b = it % 2
                smp = kind == "smp"
                ct = CT[b]; szc = SZc[b]
                if smp:
                    return (lambda c: cTs[:, c, 0:nt]), (lambda c: szs[:, c, 0:nt]), tcTs, tszs
                return (lambda c: ct[:, c, :]), (lambda c: szc[:, c, :]), tCT[b], tSZc[b]

            def C_p1(it):
                kind, i0, nt = tiles[it]
                b = it % 2
                nsub = (nt + 127) // 128
                smp = kind == "smp"
                ct = CT[b]; szc = SZc[b]; xt = XTc[it % 3]; txt = tXTc[it % 3]
                ct_v, sz_v, tct, tszc = C_views(it)
                if smp:
                    kb.dma("sp", "xtc%d" % (it % 3), xt[0:nt, 0, :], xs.ap(), writes=[txt])
                else:
                    kb.dma("sp", "ctc%d" % b, sap(ct, 0, 128, 0, [[1, NCH * TC]]), cT_d.ap()[i0 // 256],
                           reads=[DT("cT", (i0 // 256, 0)), DT("cT", (i0 // 256, 1))], writes=[tct])
                    kb.dma("sp", "szc%d" % b, sap(szc, 0, 128, 0, [[1, NCH * TC]]), szT_d.ap()[i0 // TA],
                           reads=[DT("szT", i0 // TA)], writes=[tszc])
                    kb.dma("sp", "xtc%d" % (it % 3), xt[:, 0:nsub, :], dap(xr, (PRE + i0) * D, [[D, 128], [128 * D, nsub], [1, D]]),
                           writes=[txt])
                p_sum = PF[4 * b]; p_sq = PF[4 * b + 1]
                for c in range(NCH):
                    kb.op("pe", lambda e: e.matmul(p_sum[:, 0:nt], lhsT=onesf[:, :], rhs=ct_v(c), start=(c == 0), stop=(c == NCH - 1)),
                          reads=[tct, tconst], writes=[tPF[4 * b]])
                for c in range(NCH):
                    q_ = sq[c % 2]; tq_ = tsq[c % 2]
                    kb.op("act", lambda e: e.activation(out=q_[:, 0:nt], in_=ct_v(c), func=AF.Square), reads=[tct], writes=[tq_])
                    kb.op("pe", lambda e: e.matmul(p_sq[:, 0:nt], lhsT=onesf[:, :], rhs=q_[:, 0:nt], start=(c == 0), stop=(c == NCH - 1)),
                          reads=[tq_, tconst], writes=[tPF[4 * b + 1]])

            def C_p2b(it):
                kind, i0, nt = tiles[it]
                b = it % 2
                u = Uc[b]; tu = tUc[b]
                nsub = (nt + 127) // 128
                smp = kind == "smp"
                xt = XTc[it % 3]; txt = tXTc[it % 3]
                for s in range(nsub):
                    np_ = min(128, nt - s * 128)
                    for hf in range(2):
                        pf = PF[2 + hf]; tpf = tPF[2 + hf]
                        for c in range(NCH):
                            kb.op("pe", lambda e: e.matmul(pf[0:np_, :], lhsT=u[:, c, s * 128:s * 128 + np_],
                                                           rhs=Wout[:, c, hf * 512:(hf + 1) * 512], start=(c == 0), stop=(c == NCH - 1)),
                                  reads=[tu[0], tu[1], tWout], writes=[tpf])
                        kb.op("dve", lambda e: e.tensor_tensor(out=xt[0:np_, s, hf * 512:(hf + 1) * 512],
                                                               in0=xt[0:np_, s, hf * 512:(hf + 1) * 512], in1=pf[0:np_, :], op=ALU.add),
                              reads=[tpf, txt], writes=[txt])
                if smp:
                    copy_op("dve", x1s[0:16, 0, :], xt[0:16, 0, :], [txt], [tx1s])
                else:
                    kb.dma("pool", "xtc%d" % (it % 3), dap(x1_d, i0 * D, [[D, 128], [128 * D, nsub], [1, D]]), xt[:, 0:nsub, :],
                           reads=[txt], writes=[DT("x1", i0 // TC)])


            def C_p2(it, part):
                kind, i0, nt = tiles[it]
                b = it % 2
                u = Uc[b]; tu = tUc[b]; st = STc[b]; tst = tSTc[b]
                nsub = (nt + 127) // 128
                smp = kind == "smp"
                xt = XTc[it % 3]; txt = tXTc[it % 3]
                ct_v, sz_v, tct, tszc = C_views(it)
                p_sum = PF[4 * b]; p_sq = PF[4 * b + 1]
                if part == "b":
                    return C_p2b(it)
                mean = st[:, 0, 0:nt]; rstd = st[:, 1, 0:nt]; mr = st[:, 2, 0:nt]; tmp = st[:, 3, 0:nt]
                kb.op("dve", lambda e: e.tensor_scalar(out=mean, in0=p_sum[:, 0:nt], scalar1=1.0 / CH, scalar2=None, op0=ALU.mult),
                      reads=[tPF[4 * b], tst], writes=[tst])
                kb.op("dve", lambda e: e.tensor_tensor(out=tmp, in0=mean, in1=mean, op=ALU.mult), reads=[tst], writes=[tst])
                kb.op("dve", lambda e: e.scalar_tensor_tensor(out=tmp, in0=p_sq[:, 0:nt], scalar=1.0 / CH, in1=tmp,
                                                              op0=ALU.mult, op1=ALU.subtract), reads=[tPF[4 * b + 1], tst], writes=[tst])
                kb.op("act", lambda e: e.activation(out=tmp, in_=tmp, func=AF.Sqrt, scale=1.0, bias=EPS), reads=[tst], writes=[tst])
                kb.op("dve", lambda e: e.reciprocal(out=rstd, in_=tmp), reads=[tst], writes=[tst])
                kb.op("dve", lambda e: e.tensor_tensor(out=mr, in0=mean, in1=rstd, op=ALU.mult), reads=[tst], writes=[tst])
                for c in range(NCH):
                    n_ = nrm[c % 2]; tn_ = tnrm[c % 2]; a_ = act_[c % 2]; ta_ = tact[c % 2]
                    kb.op("dve", lambda e: e.tensor_tensor(out=n_[:, 0:nt], in0=ct_v(c), in1=rstd, op=ALU.mult),
                          reads=[tct, tst], writes=[tn_])
                    kb.op("dve", lambda e: e.tensor_tensor(out=n_[:, 0:nt], in0=n_[:, 0:nt], in1=mr, op=ALU.subtract),
                          reads=[tn_, tst], writes=[tn_])
                    kb.op("act", lambda e: e.activation(out=a_[:, 0:nt], in_=n_[:, 0:nt], func=AF.Silu, scale=cw[:, c, 32:33],
                                                        bias=cw[:, c, 33:34]), reads=[tn_, tcw], writes=[ta_])
                    kb.op("dve" if c % 4 else "pool", lambda e: e.tensor_tensor(out=u[:, c, 0:nt], in0=a_[:, 0:nt], in1=sz_v(c), op=ALU.mult),
                          reads=[ta_, tszc], writes=[tu[0 if c % 4 else 1]], skip_self=True)

            nC = len(tiles)
            C_p1(0)
            for it in range(nC):
                if it + 1 < nC:
                    C_p1(it + 1)
                C_p2(it, "a")
                if it >= 1:
                    C_p2b(it - 1)
            C_p2b(nC - 1)
            kb.barrier()
        if debug == "C":
            kb.finish()
            return nc

        with ExitStack() as es:
            Wkv = sbt(es, "Wkv", [128, 8, 3072], BF16); tWkv = Tok("Wkv")
            Wb = sbt(es, "Wb", [128, 8, 2048], BF16); tWb = Tok("Wb")
            load_weights([(Wkv, tWkv, w_kv.ap(), (kv_norm, 0)), (Wb, tWb, b_w_in.ap()[0], (b_norm, 0))])
            TD = 128
            XTd = [sbt(es, "xtd%d" % i, [128, 1, D], F32) for i in range(2)]; tXTd = [Tok("xtd%d" % i) for i in range(2)]
            xn = sbt(es, "xnd", [128, 1, D], BF16); txn = Tok("xnd")
            junk = sbt(es, "junkd", [128, D], BF16); tjunk = Tok("junkd")
            hT = sbt(es, "hTd", [128, 8, TD], BF16); thT = Tok("hTd")
            KQ = [sbt(es, "kq%d" % i, [128, 6, 512], F32) for i in range(2)]; tKQ = [Tok("kq%d" % i) for i in range(2)]
            Vt = [sbt(es, "vt%d" % i, [128, 3, 512], F32) for i in range(2)]; tVt = [Tok("vt%d" % i) for i in range(2)]
            kvb = [sbt(es, "kvb%d" % i, [128, 3, 2, 512], BF16) for i in range(2)]
            tkvbK = [Tok("kvbK%d" % i) for i in range(2)]; tkvbV = [Tok("kvbV%d" % i) for i in range(2)]
            qb = [sbt(es, "qb%d" % i, [128, 3, 512], BF16) for i in range(2)]; tqb = [Tok("qb%d" % i) for i in range(2)]
            szgb = [sbt(es, "szgb%d" % i, [128, 512], BF16) for i in range(2)]; tszgb = [Tok("szgb%d" % i) for i in range(2)]
            sqh = sbt(es, "sqh", [128, 6, 512], F32); tsqh = Tok("sqh")
            HS = [sbt(es, "hs%d" % i, [128, 96], F32) for i in range(2)]; tHS = [Tok("hs%d" % i) for i in range(2)]
            RPp = [sbt(es, "rp0", [128, 4, 48 * 8], F32)] * 2; tRP = [Tok("rp0")] * 2
            cs = [sbt(es, "cs%d" % i, [128, 16], F32) for i in range(3)]; tcs = [Tok("cs%d" % i) for i in range(3)]
            gk = sbt(es, "gk", [128, 2, 512], F32); tgk = Tok("gk")
            kb.dma("sp", "gk", sap(gk, 0, 128, 0, [[64, 8], [1, 64]]), dap(k_norm, 0, [[0, 128], [0, 8], [1, 64]]), writes=[tgk])
            kb.dma("sp", "gk", sap(gk, 0, 128, 512, [[64, 8], [1, 64]]), dap(q_norm, 0, [[0, 128], [0, 8], [1, 64]]), writes=[tgk])

            tiles = [("main", t * TD, TD) for t in range(R // TD)] + [("smp", 0, 16)]
            def D_common(it):
                kind, i0, nt = tiles[it]
                b = it % 2
                smp = kind == "smp"
                is_main = (not smp) and i0 >= HALO
                withq = smp or is_main
                ns = 6 if withq else 3
                tail = (not smp) and (i0 >= R - 2048)
                return kind, i0, nt, b, smp, is_main, withq, ns, ns * 8, tail

            def D_p1(it):
                kind, i0, nt, b, smp, is_main, withq, ns, nh, tail = D_common(it)
                np_ = nt
                if smp:
                    xt = x1s; txt = tx1s
                    kb.dma("sp", "cs%d" % b, cs[b][0:16, :], css.ap(), writes=[tcs[b]])
                else:
                    xt = XTd[b]; txt = tXTd[b]
                    kb.dma("sp", "xtd%d" % b, xt[:, 0, :], x1_d.ap()[i0:i0 + 128, :], reads=[DT("x1", i0 // TC)], writes=[txt])
                    kb.dma("sp", "cs%d" % b, cs[b][:, :], csp.ap()[i0:i0 + 128, :], writes=[tcs[b]])
                rms_T(xt, txt, nt, hT, thT, xn, txn, junk, tjunk)
                kq = KQ[b]; tkq = tKQ[b]; vt = Vt[b]; tvt = tVt[b]; hs = HS[b]; ths = tHS[b]; rp = RPp[b]; trp = tRP[b]
                kbf = kvb[b]
                tail = (not smp) and (i0 >= R - 2048)
                for n in range(6):
                    g = n % 3; isv = n // 3
                    pf = PF[n % 6]; tpf = tPF[n % 6]
                    for k in range(8):
                        kb.op("pe", lambda e: e.matmul(pf[0:np_, :], lhsT=hT[:, k, 0:np_], rhs=Wkv[:, k, n * 512:(n + 1) * 512],
                                                       start=(k == 0), stop=(k == 7)), reads=[thT, tWkv], writes=[tpf])
                    if isv:
                        if not smp:
                            kb.op("act", lambda e: e.activation(out=kbf[0:np_, g, 1, :], in_=pf[0:np_, :], func=AF.Copy),
                                  reads=[tpf], writes=[tkvbV[b]], skip_self=True)
                        if tail or smp:
                            kb.op("dve", lambda e: e.tensor_copy(out=vt[0:np_, g, :], in_=pf[0:np_, :]),
                                  reads=[tpf] + ([] if smp else [tkvbV[b]]), writes=[tvt])
                    else:
                        copy_op("act" if g != 1 else "dve", kq[0:np_, g, :], pf[0:np_, :], [tpf], [tkq])
                if withq:
                    for n in range(4):
                        pf = PF[n % 6]; tpf = tPF[n % 6]
                        for k in range(8):
                            kb.op("pe", lambda e: e.matmul(pf[0:np_, :], lhsT=hT[:, k, 0:np_], rhs=Wb[:, k, n * 512:(n + 1) * 512],
                                                           start=(k == 0), stop=(k == 7)), reads=[thT, tWb], writes=[tpf])
                        if n < 3:
                            copy_op("act" if n != 1 else "dve", kq[0:np_, 3 + n, :], pf[0:np_, :], [tpf], [tkq])
                        elif smp:
                            kb.op("act", lambda e: e.activation(out=szgs[0:np_, :], in_=pf[0:np_, :], func=AF.Silu),
                                  reads=[tpf], writes=[tszgs])
                        else:
                            kb.op("act", lambda e: e.activation(out=szgb[b][0:np_, :], in_=pf[0:np_, :], func=AF.Silu),
                                  reads=[tpf], writes=[tszgb[b]])

            def D_p2(it, part=None):
                kind, i0, nt, b, smp, is_main, withq, ns, nh, tail = D_common(it)
                np_ = nt
                kq = KQ[b]; tkq = tKQ[b]; vt = Vt[b]; tvt = tVt[b]; hs = HS[b]; ths = tHS[b]; rp = RPp[b]; trp = tRP[b]
                kbf = kvb[b]
                kqv = lambda off, n: sap(kq, 0, np_, off, [[64, nh], [1, n]])
                if part in (None, "A"):
                    kb.op("act", lambda e: e.activation(out=sqh[0:np_, 0:ns, :], in_=kq[0:np_, 0:ns, :], func=AF.Square), reads=[tkq], writes=[tsqh])
                if part == "A":
                    return
                kb.op("dve", lambda e: e.tensor_reduce(out=hs[0:np_, 0:nh], in_=sap(sqh, 0, np_, 0, [[64, nh], [1, 64]]),
                                                       axis=AX.X, op=ALU.add), reads=[tsqh], writes=[ths])
                kb.op("act", lambda e: e.activation(out=hs[0:np_, 48:48 + nh], in_=hs[0:np_, 0:nh], func=AF.Sqrt, scale=1.0 / 64, bias=EPS),
                      reads=[ths], writes=[ths])
                kb.op("dve", lambda e: e.reciprocal(out=hs[0:np_, 0:nh], in_=hs[0:np_, 48:48 + nh]), reads=[ths], writes=[ths])
                kb.op("dve", lambda e: e.tensor_tensor(out=kqv(0, 64), in0=kqv(0, 64), in1=sap(hs, 0, np_, 0, [[1, nh], [0, 64]]), op=ALU.mult),
                      reads=[tkq, ths], writes=[tkq])
                kb.op("pool", lambda e: e.tensor_tensor(out=kq[0:np_, 0:3, :], in0=kq[0:np_, 0:3, :],
                                                        in1=sap(gk, 0, np_, 0, [[0, 3], [1, 512]]), op=ALU.mult), reads=[tkq, tgk], writes=[tkq])
                if withq:
                    kb.op("pool", lambda e: e.tensor_tensor(out=kq[0:np_, 3:6, :], in0=kq[0:np_, 3:6, :],
                                                            in1=sap(gk, 0, np_, 512, [[0, 3], [1, 512]]), op=ALU.mult), reads=[tkq, tgk], writes=[tkq])
                cosv = sap(cs[it % 3], 0, np_, 0, [[0, nh], [1, 8]]); sinv = sap(cs[it % 3], 0, np_, 8, [[0, nh], [1, 8]])
                r_ = lambda i: sap(rp, 0, np_, i * 384, [[8, nh], [1, 8]])
                x1v = kqv(0, 8); x2v = kqv(8, 8)
                tcst = tcs[it % 3]
                kb.op("dve", lambda e: e.tensor_tensor(out=r_(0), in0=x1v, in1=cosv, op=ALU.mult), reads=[tkq, tcst], writes=[trp])
                kb.op("dve", lambda e: e.tensor_tensor(out=r_(1), in0=x2v, in1=sinv, op=ALU.mult), reads=[tkq, tcst], writes=[trp], skip_self=True)
                kb.op("dve", lambda e: e.tensor_tensor(out=r_(2), in0=x2v, in1=cosv, op=ALU.mult), reads=[tkq, tcst], writes=[trp], skip_self=True)
                kb.op("dve", lambda e: e.tensor_tensor(out=r_(3), in0=x1v, in1=sinv, op=ALU.mult), reads=[tkq, tcst], writes=[trp], skip_self=True)
                kb.op("dve", lambda e: e.tensor_tensor(out=x1v, in0=r_(0), in1=r_(1), op=ALU.subtract), reads=[trp, tkq], writes=[tkq])
                kb.op("dve", lambda e: e.tensor_tensor(out=x2v, in0=r_(2), in1=r_(3), op=ALU.add), reads=[trp, tkq], writes=[tkq])
                if smp:
                    copy_op("pool", kvs[0:16, :, 0, :], kq[0:16, 0:3, :], [tkq], [tkvs])
                    copy_op("pool", kvs[0:16, :, 1, :], vt[0:16, :, :], [tvt, tkvs], [tkvs])
                    copy_op("pool", qs[0:16, :, :], kq[0:16, 3:6, :], [tkq], [tqs])
                else:
                    copy_op("dve", kbf[:, :, 0, :], kq[:, 0:3, :], [tkq], [tkvbK[b]])
                    kb.dma("pool", "kvb%d" % b, kv_d.ap()[i0:i0 + 128], kbf[:, :, :, :], reads=[tkvbK[b], tkvbV[b]],
                           writes=[DT("kv", i0 // 128)])
                    for g in range(3):
                        if i0 >= R - WINS[g]:
                            o0 = i0 - (R - WINS[g])
                            kb.dma("pool", "kq%d" % b, nkp[g].ap()[o0:o0 + 128, 0, :], kq[:, g, :], reads=[tkq], is_out=True)
                            kb.dma("pool", "vt%d" % b, nkp[g].ap()[o0:o0 + 128, 1, :], vt[:, g, :], reads=[tvt], is_out=True)
                    if is_main:
                        tm = i0 - HALO
                        copy_op("dve", qb[b][:, :, :], kq[:, 3:6, :], [tkq], [tqb[b]])
                        kb.dma("pool", "qb%d" % b, q_d.ap()[tm:tm + 128], qb[b][:, :, :], reads=[tqb[b]], writes=[DT("q", tm // 128)])
                        kb.dma("pool", "szgb%d" % b, szg_d.ap()[tm:tm + 128], szgb[b][:, :], reads=[tszgb[b]], writes=[DT("szg", tm // 128)])

            XN2 = [xn, sbt(es, "xnd2", [128, 1, D], BF16)]; tXN2 = [txn, Tok("xnd2")]
            SSD = [sbt(es, "ssd%d" % i, [128, 4], F32) for i in range(2)]; tSSD = [Tok("ssd%d" % i) for i in range(2)]

            def D_xt(it):
                kind, i0, nt, b, smp, is_main, withq, ns, nh, tail = D_common(it)
                return (x1s, tx1s) if smp else (XTd[b], tXTd[b])

            def D_load(it):
                kind, i0, nt, b, smp, is_main, withq, ns, nh, tail = D_common(it)
                c3 = it % 3
                if smp:
                    kb.dma("sp", "cs%d" % c3, cs[c3][0:16, :], css.ap(), writes=[tcs[c3]])
                else:
                    kb.dma("sp", "xtd%d" % b, XTd[b][:, 0, :], x1_d.ap()[i0:i0 + 128, :], reads=[DT("x1", i0 // TC)], writes=[tXTd[b]])
                    kb.dma("sp", "cs%d" % c3, cs[c3][:, :], csp.ap()[i0:i0 + 128, :], writes=[tcs[c3]])

            def D_normA(it):
                kind, i0, nt, b, smp, is_main, withq, ns, nh, tail = D_common(it)
                xt, txt = D_xt(it)
                kb.op("act", lambda e: e.activation(out=junk[0:nt, :], in_=xt[0:nt, 0, :], func=AF.Square,
                                                    accum_out=SSD[b][0:nt, 0:1]), reads=[txt], writes=[tjunk, tSSD[b]])

            def D_normB(it):
                kind, i0, nt, b, smp, is_main, withq, ns, nh, tail = D_common(it)
                xt, txt = D_xt(it)
                sd = SSD[b]; tsd = tSSD[b]
                kb.op("act", lambda e: e.activation(out=sd[0:nt, 1:2], in_=sd[0:nt, 0:1], func=AF.Sqrt, scale=1.0 / D, bias=EPS),
                      reads=[tsd], writes=[tsd])
                kb.op("dve", lambda e: e.reciprocal(out=sd[0:nt, 2:3], in_=sd[0:nt, 1:2]), reads=[tsd], writes=[tsd])
                kb.op("dve", lambda e: e.tensor_scalar(out=XN2[b][0:nt, 0, :], in0=xt[0:nt, 0, :], scalar1=sd[0:nt, 2:3],
                                                       scalar2=None, op0=ALU.mult), reads=[txt, tsd], writes=[tXN2[b]])

            def D_TT(it):
                kind, i0, nt, b, smp, is_main, withq, ns, nh, tail = D_common(it)
                for half in range(2):
                    pb, tpb = next_pb()
                    for k4 in range(4):
                        k = half * 4 + k4
                        kb.op("pe", lambda e: e.transpose(out=pb[:, k4 * 128:k4 * 128 + nt], in_=XN2[b][0:nt, 0, k * 128:(k + 1) * 128],
                                                          identity=identb[0:nt, 0:nt]), reads=[tXN2[b], tconst], writes=[tpb])
                    kb.op("act", lambda e: e.activation(out=sap(hT, 0, 128, half * 4 * TD, [[TD, 4], [1, nt]]),
                                                        in_=sap(pb, 0, 128, 0, [[128, 4], [1, nt]]), func=AF.Copy),
                          reads=[tpb], writes=[thT])

            def D_MM(it):
                kind, i0, nt, b, smp, is_main, withq, ns, nh, tail = D_common(it)
                np_ = nt
                kq = KQ[b]; tkq = tKQ[b]; vt = Vt[b]; tvt = tVt[b]
                kbf = kvb[b]
                for n in range(6):
                    g = n % 3; isv = n // 3
                    pf = PF[n % 6]; tpf = tPF[n % 6]
                    for k in range(8):
                        kb.op("pe", lambda e: e.matmul(pf[0:np_, :], lhsT=hT[:, k, 0:np_], rhs=Wkv[:, k, n * 512:(n + 1) * 512],
                                                       start=(k == 0), stop=(k == 7)), reads=[thT, tWkv], writes=[tpf])
                    if isv:
                        if not smp:
                            kb.op("act", lambda e: e.activation(out=kbf[0:np_, g, 1, :], in_=pf[0:np_, :], func=AF.Copy),
                                  reads=[tpf], writes=[tkvbV[b]], skip_self=True)
                        if tail or smp:
                            kb.op("act", lambda e: e.activation(out=vt[0:np_, g, :], in_=pf[0:np_, :], func=AF.Copy),
                                  reads=[tpf], writes=[tvt], skip_self=True)
                    else:
                        kb.op("act", lambda e: e.activation(out=kq[0:np_, g, :], in_=pf[0:np_, :], func=AF.Copy),
                              reads=[tpf], writes=[tkq], skip_self=(g > 0))
                if withq:
                    for n in range(4):
                        pf = PF[n % 6]; tpf = tPF[n % 6]
                        for k in range(8):
                            kb.op("pe", lambda e: e.matmul(pf[0:np_, :], lhsT=hT[:, k, 0:np_], rhs=Wb[:, k, n * 512:(n + 1) * 512],
                                                           start=(k == 0), stop=(k == 7)), reads=[thT, tWb], writes=[tpf])
                        if n < 3:
                            kb.op("act", lambda e: e.activation(out=kq[0:np_, 3 + n, :], in_=pf[0:np_, :], func=AF.Copy),
                                  reads=[tpf], writes=[tkq], skip_self=True)
                        elif smp:
                            kb.op("act", lambda e: e.activation(out=szgs[0:np_, :], in_=pf[0:np_, :], func=AF.Silu),
                                  reads=[tpf], writes=[tszgs])
                        else:
                            kb.op("act", lambda e: e.activation(out=szgb[b][0:np_, :], in_=pf[0:np_, :], func=AF.Silu),
                                  reads=[tpf], writes=[tszgb[b]])

            nD = len(tiles)
            D_load(0); D_load(1)
            D_normA(0); D_normB(0)
            for it in range(nD):
                if it + 1 < nD:
                    D_normA(it + 1)
                if it >= 1:
                    D_p2(it - 1, "A")
                if it + 1 < nD:
                    D_normB(it + 1)
                if it >= 1:
                    D_p2(it - 1, "B")
                D_TT(it)
                D_MM(it)
                if it + 2 < nD:
                    D_load(it + 2)
            D_p2(nD - 1, "A")
            D_p2(nD - 1, "B")
            for g in range(3):
                W = WINS[g]
                for s in range(4):
                    for r0 in range(0, W - 4, 508):
                        nr = min(508, W - 4 - r0)
                        kb.dma("sp", "d2d", nks[g].ap()[s, r0:r0 + nr],
                               dap(ck[g], (s * W + 4 + r0) * 1024, [[1024, nr], [512, 2], [1, 512]]), is_out=True)
                    kb.dma("pool", "kvs", nks[g].ap()[s, W - 4:W], kvs[4 * s:4 * s + 4, g, :, :], reads=[tkvs], is_out=True)
            kb.barrier()
        if debug == "D":
            kb.finish()
            return nc

        with ExitStack() as es:
            WoB = sbt(top, "WoB", [64, 8, D], BF16); tWoB = Tok("WoB")
            load_weights([(WoB, tWoB, b_w_out.ap()[0], None)])
            mk = sbt(es, "mk", [128, 2, 512], BF16); tmk = Tok("mk")
            mkf = sbt(es, "mkf", [128, 2, 256], F32)
            kb.dma("sp", "mk", mkf[:, 0, :], maskn_d.ap(), writes=[tmk])
            kb.dma("sp", "mk", mkf[:, 1, :], maskh_d.ap(), writes=[tmk])
            copy_op("dve", mk[:, :, 0:256], mkf[:, :, :], [tmk], [tmk])
            copy_op("dve", mk[:, :, 256:512], mkf[:, :, :], [tmk], [tmk])
            ST = 2048
            accO = sbt(es, "accO", [64, 4, ST], F32); taccO = Tok("accO")
            accD = sbt(es, "accD", [64, 4, ST], F32); taccD = Tok("accD")
            oall = sbt(es, "oall", [64, 8, ST], BF16); toall = Tok("oall")
            NB = 4
            QR = [sbt(es, "qr%d" % i, [128, 256], BF16) for i in range(NB)]; tQR = [Tok("qr%d" % i) for i in range(NB)]
            KC = [sbt(es, "kc%d" % i, [128, 2, 256], BF16) for i in range(NB)]; tKC = [Tok("kc%d" % i) for i in range(NB)]
            KP = [sbt(es, "kp%d" % i, [128, 2, 256], BF16) for i in range(NB)]; tKP = [Tok("kp%d" % i) for i in range(NB)]
            QKT = [sbt(es, "qkT%d" % i, [64, 12, 128], BF16) for i in range(2)]
            tQa = [Tok("qkTa%d" % i) for i in range(2)]; tQb = [Tok("qkTb%d" % i) for i in range(2)]
            EE = [sbt(es, "ee%d" % i, [128, 512], BF16) for i in range(2)]; tEE = [Tok("ee%d" % i) for i in range(2)]
            EM = [sbt(es, "em%d" % i, [128, 512], BF16) for i in range(2)]; tEM = [Tok("em%d" % i) for i in range(2)]
            SG = [sbt(es, "sge%d" % i, [128, 512], BF16) for i in range(2)]; tSG = [Tok("sge%d" % i) for i in range(2)]
            sgT = sbt(es, "sgT", [64, 8, 128], F32); tsgT = Tok("sgT")
            UG = [sbt(es, "ug%d" % i, [64, 8, 128], BF16) for i in range(2)]; tUG = [Tok("ug%d" % i) for i in range(2)]
            X1 = [sbt(es, "x1e%d" % i, [128, D], F32) for i in range(2)]; tX1 = [Tok("x1e%d" % i) for i in range(2)]
            cnt["blk"] = 0; cnt["unit"] = 0

            def trange(name, lo, hi):
                return [DT(name, i) for i in range(lo // 128, hi // 128 + 1)]

            for sti in range(MAIN // ST):
                T0 = HALO + sti * ST
                for hq in range(2):
                    kb.op("pool", lambda e: e.memset(accO[:, :, :], 0.0), reads=[taccO], writes=[taccO])
                    kb.op("pool", lambda e: e.memset(accD[:, :, :], 0.0), reads=[taccD], writes=[taccD])
                    pendE = [None]
                    for g in range(3):
                        d = DILS[g]
                        span = 128 * d
                        for blk in range(16):
                            base = T0 + (blk // d) * span + (blk % d)
                            off = base - T0
                            pbase = base - span
                            mki = 1 if pbase < HALO else 0
                            bn = cnt["blk"]; cnt["blk"] += 1
                            b3 = bn % NB; b = bn % 2
                            qr = QR[b3]; kc = KC[b3]; kp = KP[b3]
                            kb.dma("sp", "qr%d" % b3, qr[:, :], dap(q_d, ((base - HALO) * 3 + g) * 512 + hq * 256, [[3 * 512 * d, 128], [1, 256]]),
                                   reads=trange("q", base - HALO, base - HALO + 127 * d), writes=[tQR[b3]])
                            kb.dma("sp", "kc%d" % b3, kc[:, :, :], dap(kv_d, (base * 3 + g) * 1024 + hq * 256, [[3072 * d, 128], [512, 2], [1, 256]]),
                                   reads=trange("kv", base, base + 127 * d), writes=[tKC[b3]])
                            kb.dma("sp", "kp%d" % b3, kp[:, :, :], dap(kv_d, (pbase * 3 + g) * 1024 + hq * 256, [[3072 * d, 128], [512, 2], [1, 256]]),
                                   reads=trange("kv", pbase, pbase + 127 * d), writes=[tKP[b3]])
                            pbA, tpbA = next_pb()
                            for h4 in range(4):
                                kb.op("pe", lambda e: e.transpose(out=pbA[0:64, h4 * 128:(h4 + 1) * 128], in_=qr[:, h4 * 64:(h4 + 1) * 64],
                                                                  identity=identb[:, :]), reads=[tQR[b3], tconst], writes=[tpbA])
                            for h4 in range(4):
                                kb.op("pe", lambda e: e.transpose(out=pbA[0:64, (4 + h4) * 128:(5 + h4) * 128], in_=kc[:, 0, h4 * 64:(h4 + 1) * 64],
                                                                  identity=identb[:, :]), reads=[tKC[b3], tconst], writes=[tpbA])
                            copy_op("act", QKT[b][:, 0:8, :], sap(pbA, 0, 64, 0, [[128, 8], [1, 128]]), [tpbA], [tQa[b]])
                            pbB, tpbB = next_pb()
                            for h4 in range(4):
                                kb.op("pe", lambda e: e.transpose(out=pbB[0:64, h4 * 128:(h4 + 1) * 128], in_=kp[:, 0, h4 * 64:(h4 + 1) * 64],
                                                                  identity=identb[:, :]), reads=[tKP[b3], tconst], writes=[tpbB])
                            copy_op("dve", QKT[b][:, 8:12, :], sap(pbB, 0, 64, 0, [[128, 4], [1, 128]]), [tpbB], [tQb[b]])
                            po = PF[2 + b]; tpo = tPF[2 + b]
                            pd = PF[4 + b]; tpd = tPF[4 + b]
                            for u2 in range(2):
                                un = cnt["unit"]; cnt["unit"] += 1
                                ub = un % 2
                                psS = PF[ub]; tpsS = tPF[ub]
                                ee = EE[ub]; tee = tEE[ub]; em = EM[ub]; tem = tEM[ub]
                                for p in range(2):
                                    hh = 2 * u2 + p
                                    kb.op("pe", lambda e: e.matmul(psS[:, p * 256:p * 256 + 128], lhsT=QKT[b][:, 8 + hh, :],
                                                                   rhs=QKT[b][:, hh, :], start=True, stop=True),
                                          reads=[tQb[b], tQa[b]], writes=[tpsS])
                                    kb.op("pe", lambda e: e.matmul(psS[:, p * 256 + 128:p * 256 + 256], lhsT=QKT[b][:, 4 + hh, :],
                                                                   rhs=QKT[b][:, hh, :], start=True, stop=True),
                                          reads=[tQa[b]], writes=[tpsS])
                                kb.op("act", lambda e: e.activation(out=ee[:, :], in_=psS[:, :], func=AF.Exp, scale=SCALE),
                                      reads=[tpsS], writes=[tee])
                                kb.op("dve",
                                      lambda e: e.tensor_tensor(out=em[:, :], in0=ee[:, :], in1=mk[:, mki, :], op=ALU.mult),
                                      reads=[tee, tmk], writes=[tem])

                                def pv_phase(u2=u2, em=em, tem=tem, kp=kp, kc=kc, tkp=tKP[b3], tkc=tKC[b3], po=po, tpo=tpo, pd=pd, tpd=tpd,
                                             off=off, d=d):
                                    for p in range(2):
                                        hh = 2 * u2 + p
                                        c0 = p * 256
                                        kb.op("pe", lambda e: e.matmul(po[0:64, hh * 128:(hh + 1) * 128], lhsT=kp[:, 1, hh * 64:(hh + 1) * 64],
                                                                       rhs=em[:, c0:c0 + 128], start=True, stop=False), reads=[tkp, tem], writes=[tpo])
                                        kb.op("pe", lambda e: e.matmul(po[0:64, hh * 128:(hh + 1) * 128], lhsT=kc[:, 1, hh * 64:(hh + 1) * 64],
                                                                       rhs=em[:, c0 + 128:c0 + 256], start=False, stop=True), reads=[tkc, tem], writes=[tpo])
                                        kb.op("pe", lambda e: e.matmul(pd[0:64, hh * 128:(hh + 1) * 128], lhsT=onesb[:, :],
                                                                       rhs=em[:, c0:c0 + 128], start=True, stop=False), reads=[tconst, tem], writes=[tpd])
                                        kb.op("pe", lambda e: e.matmul(pd[0:64, hh * 128:(hh + 1) * 128], lhsT=onesb[:, :],
                                                                       rhs=em[:, c0 + 128:c0 + 256], start=False, stop=True), reads=[tconst, tem], writes=[tpd])
                                    if u2 == 1:
                                        av = lambda t_: sap(t_, 0, 64, off, [[ST, 4], [d, 128]])
                                        kb.op("dve", lambda e: e.tensor_tensor(out=av(accO), in0=av(accO), in1=sap(po, 0, 64, 0, [[128, 4], [1, 128]]),
                                                                               op=ALU.add), reads=[tpo, taccO], writes=[taccO])
                                        kb.op("dve", lambda e: e.tensor_tensor(out=av(accD), in0=av(accD), in1=sap(pd, 0, 64, 0, [[128, 4], [1, 128]]),
                                                                               op=ALU.add), reads=[tpd, taccD], writes=[taccD])

                                if KE_PIPE:
                                    if pendE[0] is not None:
                                        pendE[0]()
                                    pendE[0] = pv_phase
                                else:
                                    pv_phase()
                    if pendE[0] is not None:
                        pendE[0]()
                    pendE[0] = None
                    kb.op("dve", lambda e: e.reciprocal(out=accD[:, :, :], in_=accD[:, :, :]), reads=[taccD], writes=[taccD])
                    kb.op("dve", lambda e: e.tensor_tensor(out=oall[:, hq * 4:(hq + 1) * 4, :], in0=accO[:, :, :], in1=accD[:, :, :], op=ALU.mult),
                          reads=[taccD, taccO, toall], writes=[toall])

                def O_p1(sub):
                    tm = sti * ST + sub * 128
                    b = sub % 2
                    kb.dma("sp", "sge%d" % b, SG[b][:, :], szg_d.ap()[tm:tm + 128], reads=[DT("szg", tm // 128)], writes=[tSG[b]])
                    kb.dma("sp", "x1e%d" % b, X1[b][:, :], x1_d.ap()[HALO + tm:HALO + tm + 128], reads=[DT("x1", (HALO + tm) // TC)],
                           writes=[tX1[b]])
                    pb, tpb = next_pb()
                    for h in range(8):
                        kb.op("pe", lambda e: e.transpose(out=pb[0:64, h * 128:(h + 1) * 128], in_=SG[b][:, h * 64:(h + 1) * 64],
                                                          identity=identb[:, :]), reads=[tSG[b], tconst], writes=[tpb])
                    copy_op("act", sgT[:, :, :], sap(pb, 0, 64, 0, [[128, 8], [1, 128]]), [tpb], [tsgT])
                    kb.op("dve", lambda e: e.tensor_tensor(out=UG[b][:, :, :], in0=oall[:, :, sub * 128:(sub + 1) * 128], in1=sgT[:, :, :],
                                                           op=ALU.mult), reads=[toall, tsgT], writes=[tUG[b]])

                def O_p2(sub):
                    tm = sti * ST + sub * 128
                    b = sub % 2
                    for hf in range(2):
                        pf = PF[hf]; tpf = tPF[hf]
                        for h in range(8):
                            kb.op("pe", lambda e: e.matmul(pf[:, :], lhsT=UG[b][:, h, :], rhs=WoB[:, h, hf * 512:(hf + 1) * 512],
                                                           start=(h == 0), stop=(h == 7)), reads=[tUG[b], tWoB], writes=[tpf])
                        kb.op("dve", lambda e: e.tensor_tensor(out=X1[b][:, hf * 512:(hf + 1) * 512], in0=X1[b][:, hf * 512:(hf + 1) * 512],
                                                               in1=pf[:, :], op=ALU.add), reads=[tpf, tX1[b]], writes=[tX1[b]])
                    kb.dma("pool", "x1e%d" % b, yp.ap()[tm:tm + 128], X1[b][:, :], reads=[tX1[b]], is_out=True)

                for sub in range(ST // 128):
                    O_p1(sub)
                    O_p2(sub)
            kb.barrier()
        if debug == "E":
            kb.finish()
            return nc

        with ExitStack() as es:
            smk = sbt(es, "smk", [128, 4], F32); nmk = sbt(es, "nmk", [16, 48], F32)
            selr = sbt(es, "selr", [16, 16, 128], F32); selc = sbt(es, "selc", [128, 16, 16], F32)
            tsc = Tok("sconst")
            kb.dma("sp", "sconst", smk[:, :], smask_d.ap(), writes=[tsc])
            kb.dma("sp", "sconst", nmk[:, :], nmask_d.ap(), writes=[tsc])
            kb.dma("sp", "sconst", selr[:, :, :], selr_d.ap().rearrange("p (a b) -> p a b", a=16), writes=[tsc])
            kb.dma("sp", "sconst", selc[:, :, :], selc_d.ap().rearrange("p (a b) -> p a b", a=16), writes=[tsc])
            KVC = [sbt(es, "kvc%d" % i, [128, 2, 512], F32) for i in range(2)]; tKVC = [Tok("kvc%d" % i) for i in range(2)]
            prod = sbt(es, "prod", [128, 512], F32); tprod = Tok("prod")
            ev = sbt(es, "ev", [128, 520], F32); tev = Tok("ev")
            prodn = sbt(es, "prodn", [16, 512], F32); tprodn = Tok("prodn")
            evn = sbt(es, "evn", [16, 520], F32); tevn = Tok("evn")
            sS = sbt(es, "sS", [128, 16], F32); tsS = Tok("sS")
            sN = sbt(es, "sN", [16, 16], F32); tsN = Tok("sN")
            accs = sbt(es, "accs", [16, 520], F32); taccs = Tok("accs")
            kb.op("dve", lambda e: e.memset(accs[:, :], 0.0), writes=[taccs])
            li = 0
            for g in range(3):
                d = DILS[g]; W = WINS[g]
                for s in range(4):
                    for t in range(4):
                        row = 4 * s + t
                        if g == 0:
                            if t == 0:
                                li += 1
                                kb.dma("sp", "kvc%d" % (li % 2), KVC[li % 2][:, :, :],
                                       dap(ck[0], s * W * 1024, [[1024, 128], [512, 2], [1, 512]]), writes=[tKVC[li % 2]])
                        else:
                            li += 1
                            kb.dma("sp", "kvc%d" % (li % 2), KVC[li % 2][:, :, :],
                                   dap(ck[g], (s * W + t) * 1024, [[1024 * d, 128], [512, 2], [1, 512]]), writes=[tKVC[li % 2]])
                        kvc = KVC[li % 2]; tkvc = tKVC[li % 2]
                        pq = PF[0]; tpq = tPF[0]
                        kb.op("pe", lambda e: e.matmul(pq[:, :], lhsT=selr[:, row, :], rhs=qs[0:16, g, :], start=True, stop=True),
                              reads=[tsc, tqs], writes=[tpq])
                        kb.op("dve", lambda e: e.tensor_tensor(out=prod[:, :], in0=kvc[:, 0, :], in1=pq[:, :], op=ALU.mult),
                              reads=[tkvc, tpq, tprod], writes=[tprod])
                        kb.op("dve", lambda e: e.tensor_reduce(out=sS[:, 0:8], in_=sap(prod, 0, 128, 0, [[64, 8], [1, 64]]), axis=AX.X, op=ALU.add),
                              reads=[tprod, tsS], writes=[tsS])
                        kb.op("act", lambda e: e.activation(out=ev[:, 512:520], in_=sS[:, 0:8], func=AF.Exp, scale=SCALE),
                              reads=[tsS, tev], writes=[tev])
                        if g == 0:
                            kb.op("dve", lambda e: e.tensor_scalar(out=ev[:, 512:520], in0=ev[:, 512:520], scalar1=smk[:, t:t + 1],
                                                                   scalar2=None, op0=ALU.mult), reads=[tev, tsc], writes=[tev])
                        kb.op("dve", lambda e: e.tensor_tensor(out=sap(ev, 0, 128, 0, [[64, 8], [1, 64]]), in0=sap(kvc, 0, 128, 512, [[64, 8], [1, 64]]),
                                                               in1=sap(ev, 0, 128, 512, [[1, 8], [0, 64]]), op=ALU.mult),
                              reads=[tkvc, tev], writes=[tev])
                        kb.op("dve", lambda e: e.tensor_tensor(out=prodn[:, :], in0=kvs[0:16, g, 0, :], in1=pq[0:16, :], op=ALU.mult),
                              reads=[tkvs, tpq, tprodn], writes=[tprodn])
                        kb.op("dve", lambda e: e.tensor_reduce(out=sN[:, 0:8], in_=sap(prodn, 0, 16, 0, [[64, 8], [1, 64]]), axis=AX.X, op=ALU.add),
                              reads=[tprodn, tsN], writes=[tsN])
                        kb.op("act", lambda e: e.activation(out=evn[:, 512:520], in_=sN[:, 0:8], func=AF.Exp, scale=SCALE),
                              reads=[tsN, tevn], writes=[tevn])
                        mcol = (0 if g == 0 else 16) + row
                        kb.op("dve", lambda e: e.tensor_scalar(out=evn[:, 512:520], in0=evn[:, 512:520], scalar1=nmk[:, mcol:mcol + 1],
                                                               scalar2=None, op0=ALU.mult), reads=[tevn, tsc], writes=[tevn])
                        kb.op("dve", lambda e: e.tensor_tensor(out=sap(evn, 0, 16, 0, [[64, 8], [1, 64]]), in0=sap(kvs, 0, 16, (g * 2 + 1) * 512, [[64, 8], [1, 64]]),
                                                               in1=sap(evn, 0, 16, 512, [[1, 8], [0, 64]]), op=ALU.mult),
                              reads=[tkvs, tevn], writes=[tevn])
                        po = PF[2]; tpo = tPF[2]; pd = PF[3]; tpd = tPF[3]
                        kb.op("pe", lambda e: e.matmul(po[0:16, :], lhsT=selc[:, row, :], rhs=ev[:, 0:512], start=True, stop=False),
                              reads=[tsc, tev], writes=[tpo])
                        kb.op("pe", lambda e: e.matmul(po[0:16, :], lhsT=selc[0:16, row, :], rhs=evn[:, 0:512], start=False, stop=True),
                              reads=[tsc, tevn], writes=[tpo])
                        kb.op("pe", lambda e: e.matmul(pd[0:16, 0:8], lhsT=selc[:, row, :], rhs=ev[:, 512:520], start=True, stop=False),
                              reads=[tsc, tev], writes=[tpd])
                        kb.op("pe", lambda e: e.matmul(pd[0:16, 0:8], lhsT=selc[0:16, row, :], rhs=evn[:, 512:520], start=False, stop=True),
                              reads=[tsc, tevn], writes=[tpd])
                        kb.op("dve", lambda e: e.tensor_tensor(out=accs[:, 0:512], in0=accs[:, 0:512], in1=po[0:16, :], op=ALU.add),
                              reads=[tpo, taccs], writes=[taccs])
                        kb.op("dve", lambda e: e.tensor_tensor(out=accs[:, 512:520], in0=accs[:, 512:520], in1=pd[0:16, 0:8], op=ALU.add),
                              reads=[tpd, taccs], writes=[taccs])
            kb.op("dve", lambda e: e.reciprocal(out=accs[:, 512:520], in_=accs[:, 512:520]), reads=[taccs], writes=[taccs])
            kb.op("dve", lambda e: e.tensor_tensor(out=sap(accs, 0, 16, 0, [[64, 8], [1, 64]]), in0=sap(accs, 0, 16, 0, [[64, 8], [1, 64]]),
                                                   in1=sap(accs, 0, 16, 512, [[1, 8], [0, 64]]), op=ALU.mult), reads=[taccs], writes=[taccs])
            ugs = sbt(es, "ugs", [16, 512], BF16); tugs = Tok("ugs")
            kb.op("dve", lambda e: e.tensor_tensor(out=ugs[:, :], in0=accs[:, 0:512], in1=szgs[0:16, :], op=ALU.mult),
                  reads=[taccs, tszgs], writes=[tugs])
            pb, tpb = next_pb()
            for h in range(8):
                kb.op("pe", lambda e: e.transpose(out=pb[0:64, h * 16:(h + 1) * 16], in_=ugs[0:16, h * 64:(h + 1) * 64],
                                                  identity=identb[0:16, 0:16]), reads=[tugs, tconst], writes=[tpb])
            ugsT = sbt(es, "ugsT", [64, 8, 16], BF16); tugsT = Tok("ugsT")
            copy_op("dve", ugsT[:, :, :], sap(pb, 0, 64, 0, [[16, 8], [1, 16]]), [tpb], [tugsT])
            for hf in range(2):
                pf = PF[hf]; tpf = tPF[hf]
                for h in range(8):
                    kb.op("pe", lambda e: e.matmul(pf[0:16, :], lhsT=ugsT[:, h, :], rhs=WoB[:, h, hf * 512:(hf + 1) * 512],
                                                   start=(h == 0), stop=(h == 7)), reads=[tugsT, tWoB], writes=[tpf])
                kb.op("dve", lambda e: e.tensor_tensor(out=x1s[0:16, 0, hf * 512:(hf + 1) * 512], in0=x1s[0:16, 0, hf * 512:(hf + 1) * 512],
                                                       in1=pf[0:16, :], op=ALU.add), reads=[tpf, tx1s], writes=[tx1s])
            kb.dma("pool", "x1s", ys.ap(), x1s[0:16, 0, :], reads=[tx1s], is_out=True)
            kb.barrier()
        kb.finish()
    return nc


_NC_CACHE = {}


def _rope_table(pos):
    half = 8
    inv_freq = np.exp(-math.log(500000.0) * np.arange(half, dtype=np.float32) * np.float32(2.0 / 16)).astype(np.float32)
    ang = pos.astype(np.float32)[:, None] * inv_freq[None, :]
    return np.concatenate([np.cos(ang), np.sin(ang)], axis=1).astype(np.float32)


def make_in_maps(inputs):
    f = lambda a: np.ascontiguousarray(np.asarray(a, dtype=np.float32))
    x_prompt = f(inputs["x_prompt"]); x_sample = f(inputs["x_sample"])
    shared = {k: f(inputs[k]) for k in ("a_norm", "a_w_in", "a_conv_w", "a_conv_b", "a_ln_g", "a_ln_b", "a_w_out", "kv_norm",
                                        "w_kv", "k_norm", "b_norm", "b_w_in", "q_norm", "b_w_out")}
    kj = np.arange(128)[:, None]; qi = np.arange(128)[None, :]
    maskp = (kj >= qi).astype(np.float32); maskc = (kj <= qi).astype(np.float32)
    maskn = np.concatenate([maskp, maskc], axis=1)
    smask = (np.arange(128)[:, None] >= np.arange(4)[None, :]).astype(np.float32)
    nm = np.zeros((16, 48), np.float32)
    for s in range(4):
        for t in range(4):
            for t2 in range(4):
                nm[4 * s + t2, 4 * s + t] = 1.0 if t2 <= t else 0.0
                nm[4 * s + t2, 16 + 4 * s + t] = 1.0 if t2 == t else 0.0
    selr = np.zeros((16, 16, 128), np.float32)
    selc = np.zeros((128, 16, 16), np.float32)
    for r in range(16):
        selr[r, r, :] = 1.0
        selc[:, r, r] = 1.0
    shared.update(identf=np.eye(128, dtype=np.float32), maskn=maskn, smask=smask, nmask=nm,
                  selr=selr.reshape(16, -1), selc=selc.reshape(128, -1),
                  css=_rope_table(PAST + np.arange(16) % 4))
    in_maps = []
    for c in range(8):
        b, h = c // 2, c % 2
        start = h * MAIN
        xr = np.zeros((RP, D), np.float32)
        lo = start - HALO - PRE
        if lo >= 0:
            xr[:] = x_prompt[b, lo:start + MAIN]
        else:
            xr[-lo:] = x_prompt[b, 0:start + MAIN]
        m = dict(shared)
        m["xr"] = xr
        m["xs"] = np.ascontiguousarray(x_sample[4 * c:4 * c + 4].reshape(16, D))
        m["cconv"] = f(inputs["cache_conv"][0, 4 * c:4 * c + 4])
        for g, nm_ in enumerate(("cache_kv_w128", "cache_kv_w512", "cache_kv_w2048")):
            m["ck%d" % g] = f(inputs[nm_][4 * c:4 * c + 4])
        m["csp"] = _rope_table(start - HALO + np.arange(R))
        mh = maskn.copy()
        if h == 0:
            mh[:, 0:128] = 0.0
        m["maskh"] = mh
        in_maps.append(m)
    return in_maps


def kernel(**inputs):
    if "nc" not in _NC_CACHE:
        _NC_CACHE["nc"] = build()
    nc = _NC_CACHE["nc"]
    in_maps = make_in_maps(inputs)
    res = run_bass_kernel_spmd(nc, in_maps, core_ids=list(range(8)))
    rs = res.results
    y_prompt = np.stack([np.concatenate([rs[2 * b]["yp"], rs[2 * b + 1]["yp"]], axis=0) for b in range(4)], axis=0)
    y_sample = np.concatenate([rs[c]["ys"].reshape(4, 4, D) for c in range(8)], axis=0)
    ncp = np.stack([rs[2 * b + 1]["ncp"][2:32] for b in range(4)], axis=0)[None]
    ncs = np.concatenate([rs[c]["ncs"] for c in range(8)], axis=0)[None]
    outs = [y_prompt, y_sample, ncp, ncs]
    for g in range(3):
        W = WINS[g]
        outs.append(np.stack([rs[2 * b + 1]["nk%dp" % g].reshape(W, 2, 8, 64) for b in range(4)], axis=0))
        outs.append(np.concatenate([rs[c]["nk%ds" % g].reshape(4, W, 2, 8, 64) for c in range(8)], axis=0))
    return tuple(np.ascontiguousarray(o.astype(np.float32)) for o in outs)
```

```python
import math
import os
KSKEW = os.environ.get('KSKEW', 'ACD')
KE_PIPE = os.environ.get('KE_PIPE', '1') == '1'
from contextlib import ExitStack
import numpy as np
import concourse.bass as bass
import concourse.mybir as mybir
from concourse.bass_utils import run_bass_kernel_spmd

F32 = mybir.dt.float32
BF16 = mybir.dt.bfloat16
AF = mybir.ActivationFunctionType
ALU = mybir.AluOpType
AX = mybir.AxisListType

D = 1024
CH = 2048
NCH = 16
CW = 31
PRE = 32
HALO = 2048
MAIN = 4096
R = HALO + MAIN
RP = PRE + R
TA = 256
EPS = 1e-6
SCALE = 64 ** -0.5
WINS = (128, 512, 2048)
DILS = (1, 4, 16)
PAST = 16384


class Tok:
    __slots__ = ("name", "w", "r")

    def __init__(self, name):
        self.name = name
        self.w = {}
        self.r = {}


class KB:
    def __init__(self, nc):
        self.nc = nc
        self.engs = {"pe": nc.tensor, "act": nc.scalar, "dve": nc.vector, "pool": nc.gpsimd, "sp": nc.sync}
        self.sems = {}
        self.total = {}
        self.isdma = {}
        self.seen = {e: {} for e in self.engs}
        self.cur = {}
        self.outh = {}
        self.nops = 0
        for e in ("pe", "act", "dve", "pool"):
            self.new_eng_sem(e, 0)

    def new_eng_sem(self, e, gen):
        key = "%s%d" % (e, gen)
        self.sems[key] = self.nc.alloc_semaphore(name="s_" + key)
        self.total[key] = 0
        self.isdma[key] = False
        self.cur[e] = key

    def dsem(self, name):
        key = "d_" + name
        if key not in self.sems:
            self.sems[key] = self.nc.alloc_semaphore(name=key)
            self.total[key] = 0
            self.isdma[key] = True
        return key

    def _wait(self, eng, deps, skip_self=False):
        for key, cnt in deps.items():
            if self.isdma[key]:
                cnt = self.total[key]
            elif eng == "pe" and key.startswith("pe"):
                continue
            elif skip_self and key.startswith(eng):
                continue
            if self.seen[eng].get(key, 0) >= cnt:
                continue
            self.engs[eng].wait_ge(self.sems[key], cnt)
            self.seen[eng][key] = cnt

    @staticmethod
    def _merge(d, o):
        for k, v in o.items():
            if d.get(k, 0) < v:
                d[k] = v

    def _deps(self, reads, writes):
        deps = {}
        for t in reads:
            self._merge(deps, t.w)
        for t in writes:
            self._merge(deps, t.w)
            self._merge(deps, t.r)
        return deps

    def op(self, eng, fn, reads=(), writes=(), skip_self=False):
        self._wait(eng, self._deps(reads, writes), skip_self)
        ins = fn(self.engs[eng])
        key = self.cur[eng]
        self.total[key] += 1
        ins.then_inc(self.sems[key], 1)
        h = {key: self.total[key]}
        for t in reads:
            self._merge(t.r, h)
        for t in writes:
            t.w = dict(h)
            t.r = {}
        self.nops += 1
        return h

    def dma(self, q, sem, out, in_, reads=(), writes=(), is_out=False, slow=False):
        self._wait(q, self._deps(reads, writes))
        key = self.dsem(sem)
        ins = self.engs[q].dma_start(out=out, in_=in_, allow_slow_non_contiguous=True) if slow else self.engs[q].dma_start(out=out, in_=in_)
        self.total[key] += 16
        ins.then_inc(self.sems[key], 16)
        h = {key: self.total[key]}
        for t in reads:
            self._merge(t.r, h)
        for t in writes:
            t.w = dict(h)
            t.r = {}
        if is_out:
            self._merge(self.outh, h)
        self.nops += 1
        return h

    def barrier(self):
        alls = {k: v for k, v in self.total.items() if v > 0}
        for e in self.engs:
            self._wait(e, {k: v for k, v in alls.items() if not (e != "pe" and False)})

    def finish(self):
        alls = {k: v for k, v in self.total.items() if v > 0 and self.isdma[k]}
        self._wait("sp", alls)


def sap(t, p0, np_, off, dims):
    ps = 1
    for s in t.shape[1:]:
        ps *= s
    return bass.AP(t, p0 * ps + off, [[ps, np_]] + [list(d) for d in dims])


def dap(t, off, dims):
    return bass.AP(t, off, [list(d) for d in dims])


def build(debug=False):
    nc = bass.Bass("TRN2", target_bir_lowering=False)
    kb = KB(nc)
    din = lambda n, s: nc.dram_tensor(n, list(s), F32, kind="ExternalInput")
    dout = lambda n, s: nc.dram_tensor(n, list(s), F32, kind="ExternalOutput")
    dscr = lambda n, s, dt: nc.dram_tensor(n, list(s), dt, kind=("ExternalOutput" if debug else "Internal"))

    xr = din("xr", [RP, D]); xs = din("xs", [16, D]); cconv = din("cconv", [4, 30, CH])
    ck = [din("ck%d" % g, [4, WINS[g], 2, 8, 64]) for g in range(3)]
    a_norm = din("a_norm", [1, D]); a_w_in = din("a_w_in", [1, D, 3 * CH]); a_conv_w = din("a_conv_w", [1, CW, CH])
    a_conv_b = din("a_conv_b", [1, CH]); a_ln_g = din("a_ln_g", [1, CH]); a_ln_b = din("a_ln_b", [1, CH])
    a_w_out = din("a_w_out", [1, CH, D]); kv_norm = din("kv_norm", [D]); w_kv = din("w_kv", [D, 3072])
    k_norm = din("k_norm", [64]); b_norm = din("b_norm", [1, D]); b_w_in = din("b_w_in", [1, D, 2048])
    q_norm = din("q_norm", [1, 64]); b_w_out = din("b_w_out", [1, 512, D])
    csp = din("csp", [R, 16]); css = din("css", [16, 16])
    identf_d = din("identf", [128, 128]); maskn_d = din("maskn", [128, 256]); maskh_d = din("maskh", [128, 256])
    smask_d = din("smask", [128, 4]); nmask_d = din("nmask", [16, 48]); selr_d = din("selr", [16, 16 * 128])
    selc_d = din("selc", [128, 16 * 16])

    yp = dout("yp", [MAIN, D]); ys = dout("ys", [16, D]); ncp = dout("ncp", [32, CH]); ncs = dout("ncs", [4, 30, CH])
    nkp = [dout("nk%dp" % g, [WINS[g], 2, 512]) for g in range(3)]
    nks = [dout("nk%ds" % g, [4, WINS[g], 2, 512]) for g in range(3)]

    vT_d = dscr("vT_d", [R // TA, 128, NCH * TA], BF16); vpre_d = dscr("vpre_d", [128, NCH * PRE], BF16)
    szT_d = dscr("szT_d", [R // TA, 128, NCH * TA], BF16); cT_d = dscr("cT_d", [R // 256, 128, NCH * 256], F32)
    x1_d = dscr("x1_d", [R, D], F32); kv_d = dscr("kv_d", [R, 3, 2, 512], BF16); q_d = dscr("q_d", [MAIN, 3, 512], BF16)
    szg_d = dscr("szg_d", [MAIN, 512], BF16)
    dtok = {}

    def DT(name, i):
        k = (name, i)
        if k not in dtok:
            dtok[k] = Tok("%s_%s" % (name, i))
        return dtok[k]

    uniq = [0]

    def sbt(es, n, s, d):
        uniq[0] += 1
        return es.enter_context(nc.sbuf_tensor("sb%d_%s" % (uniq[0], n), list(s), d))

    pst = lambda es, n, s, d: es.enter_context(nc.psum_tensor("ps_" + n, list(s), d))

    with ExitStack() as top:
        PF = [pst(top, "pf%d" % i, [128, 512], F32) for i in range(6)]
        PB = [pst(top, "pb%d" % i, [128, 1024], BF16) for i in range(2)]
        tPF = [Tok("pf%d" % i) for i in range(6)]
        tPB = [Tok("pb%d" % i) for i in range(2)]
        identf = sbt(top, "identf", [128, 128], F32); identb = sbt(top, "identb", [128, 128], BF16)
        onesf = sbt(top, "onesf", [128, 128], F32); onesb = sbt(top, "onesb", [128, 64], BF16)
        tconst = Tok("const")
        vTs = sbt(top, "vTs", [128, NCH, 16], BF16); szs = sbt(top, "szs", [128, NCH, 16], BF16)
        cTs = sbt(top, "cTs", [128, NCH, 16], F32)
        tvTs = Tok("vTs"); tszs = Tok("szs"); tcTs = Tok("cTs")
        x1s = sbt(top, "x1s", [16, 1, D], F32); tx1s = Tok("x1s")
        kvs = sbt(top, "kvs", [16, 3, 2, 512], F32); tkvs = Tok("kvs")
        qs = sbt(top, "qs", [16, 3, 512], F32); tqs = Tok("qs")
        szgs = sbt(top, "szgs", [16, 512], F32); tszgs = Tok("szgs")
        ss = sbt(top, "ss", [128, 8], F32); tss = Tok("ss")
        cw = sbt(top, "cw", [128, NCH, 34], F32); tcw = Tok("cw")

        kb.dma("sp", "const", identf[:], identf_d.ap(), writes=[tconst])
        kb.op("dve", lambda e: e.tensor_copy(out=identb[:], in_=identf[:]), reads=[tconst], writes=[tconst])
        kb.op("dve", lambda e: e.memset(onesf[:], 1.0), writes=[tconst])
        kb.op("dve", lambda e: e.memset(onesb[:], 1.0), writes=[tconst])

        cnt = {"cv": 0, "pb": 0}

        def evac_eng():
            cnt["cv"] += 1
            return "act" if cnt["cv"] % 2 else "dve"

        def next_pb():
            cnt["pb"] += 1
            return PB[cnt["pb"] % 2], tPB[cnt["pb"] % 2]

        def copy_op(eng, out, in_, reads, writes):
            if eng == "act":
                return kb.op("act", lambda e: e.activation(out=out, in_=in_, func=AF.Copy), reads=reads, writes=writes)
            return kb.op(eng, lambda e: e.tensor_copy(out=out, in_=in_), reads=reads, writes=writes)

        def load_weights(specs):
            with ExitStack() as es2:
                stg = [sbt(es2, "stg%d" % i, [128, 2048], F32) for i in range(6)]
                tstg = [Tok("stg%d" % i) for i in range(6)]
                gts = []
                for wi, (W, tW, src, gain) in enumerate(specs):
                    kparts, nk, cols = W.shape[0], W.shape[1], W.shape[2]
                    g = None
                    if gain is not None:
                        g = sbt(es2, "wg%d" % wi, [kparts, nk], F32)
                        kb.dma("sp", "wg", sap(g, 0, kparts, 0, [[1, nk], [1, 1]]),
                               dap(gain[0], gain[1], [[1, kparts], [kparts, nk], [1, 1]]), writes=[tW], slow=True)
                    gts.append(g)
                i = 0
                for wi, (W, tW, src, gain) in enumerate(specs):
                    kparts, nk, cols = W.shape[0], W.shape[1], W.shape[2]
                    g = gts[wi]
                    for k in range(nk):
                        for c0 in range(0, cols, 2048):
                            cwid = min(2048, cols - c0)
                            b = i % 6
                            kb.dma("sp" if i % 2 == 0 else "act", "stg%d" % b, stg[b][0:kparts, 0:cwid],
                                   src[k * kparts:(k + 1) * kparts, c0:c0 + cwid], writes=[tstg[b]])
                            eng = ("act", "dve", "act", "dve", "pool")[i % 5]
                            o = W[:, k, c0:c0 + cwid]
                            in_ = stg[b][0:kparts, 0:cwid]
                            if g is not None:
                                sc = g[:, k:k + 1]
                                if eng == "act":
                                    kb.op("act", lambda e: e.activation(out=o, in_=in_, func=AF.Copy, scale=sc),
                                          reads=[tstg[b], tW], writes=[tW])
                                else:
                                    kb.op(eng, lambda e: e.tensor_scalar(out=o, in0=in_, scalar1=sc, scalar2=None, op0=ALU.mult),
                                          reads=[tstg[b], tW], writes=[tW])
                            else:
                                copy_op(eng, o, in_, [tstg[b], tW], [tW])
                            i += 1
                kb.barrier()

        def rms_T(xt, txt, nt, hT, thT, xn, txn, junk, tjunk):
            nsub = (nt + 127) // 128
            for s in range(nsub):
                np_ = min(128, nt - s * 128)
                kb.op("act", lambda e: e.activation(out=junk[0:np_, :], in_=xt[0:np_, s, :], func=AF.Square,
                                                    accum_out=ss[0:np_, s:s + 1]), reads=[txt], writes=[tjunk, tss])
            npm = min(128, nt)
            kb.op("act", lambda e: e.activation(out=ss[0:npm, 4:4 + nsub], in_=ss[0:npm, 0:nsub], func=AF.Sqrt,
                                                scale=1.0 / D, bias=EPS), reads=[tss], writes=[tss])
            kb.op("dve", lambda e: e.reciprocal(out=ss[0:npm, 0:nsub], in_=ss[0:npm, 4:4 + nsub]), reads=[tss], writes=[tss])
            for s in range(nsub):
                np_ = min(128, nt - s * 128)
                eng = "dve" if s % 2 == 0 else "pool"
                kb.op(eng, lambda e: e.tensor_scalar(out=xn[0:np_, s, :], in0=xt[0:np_, s, :], scalar1=ss[0:np_, s:s + 1],
                                                     scalar2=None, op0=ALU.mult), reads=[txt, tss], writes=[txn])
            for k in range(8):
                pb, tpb = next_pb()
                for s in range(nsub):
                    np_ = min(128, nt - s * 128)
                    kb.op("pe", lambda e: e.transpose(out=pb[:, s * 128:s * 128 + np_], in_=xn[0:np_, s, k * 128:(k + 1) * 128],
                                                      identity=identb[0:np_, 0:np_]), reads=[txn, tconst], writes=[tpb])
                copy_op(evac_eng(), hT[:, k, 0:nt], pb[:, 0:nt], [tpb], [thT])

        with ExitStack() as es:
            Win = sbt(es, "Win", [128, 8, 3 * CH], BF16); tWin = Tok("Win")
            load_weights([(Win, tWin, a_w_in.ap()[0], (a_norm, 0))])
            if debug == "W":
                kb.finish()
                return nc
            XT = [sbt(es, "xtA%d" % i, [128, TA // 128, D], F32) for i in range(2)]; tXT = [Tok("xtA%d" % i) for i in range(2)]
            xn = sbt(es, "xnA", [128, TA // 128, D], BF16); txn = Tok("xnA")
            junk = sbt(es, "junkA", [128, D], BF16); tjunk = Tok("junkA")
            HTa = [sbt(es, "hTA%d" % i, [128, 8, TA], BF16) for i in range(2)]; tHTa = [Tok("hTA%d" % i) for i in range(2)]
            VA = [sbt(es, "vallA%d" % i, [128, NCH, TA], BF16) for i in range(2)]; tVA = [Tok("vallA%d" % i) for i in range(2)]
            SZ = [sbt(es, "szallA%d" % i, [128, NCH, TA], BF16) for i in range(2)]; tSZ = [Tok("szallA%d" % i) for i in range(2)]
            sg = [sbt(es, "sgA%d" % i, [128, 512], F32) for i in range(2)]; tsg = [Tok("sgA%d" % i) for i in range(2)]
            vtm = sbt(es, "vtm", [32, CH], F32); tvtm = Tok("vtm")

            tiles = [("pre", 0, PRE)] + [("main", PRE + t * TA, TA) for t in range(R // TA)] + [("smp", 0, 16)]
            XNa = [xn, sbt(es, "xnA2", [128, TA // 128, D], BF16)]; tXNa = [txn, Tok("xnA2")]
            SSa = [sbt(es, "ssA%d" % i, [128, 8], F32) for i in range(2)]; tSSa = [Tok("ssA%d" % i) for i in range(2)]

            def A_load(it):
                kind, r0, nt = tiles[it]
                xt = XT[it % 2]; txt = tXT[it % 2]
                nsub = (nt + 127) // 128
                if kind == "smp":
                    kb.dma("sp", "xtA%d" % (it % 2), xt[0:nt, 0, :], xs.ap(), writes=[txt])
                elif nt < 128:
                    kb.dma("sp", "xtA%d" % (it % 2), xt[0:nt, 0, :], xr.ap()[r0:r0 + nt, :], writes=[txt])
                else:
                    kb.dma("sp", "xtA%d" % (it % 2), xt[:, 0:nsub, :],
                           dap(xr, r0 * D, [[D, 128], [128 * D, nsub], [1, D]]), writes=[txt])

            def A_norm(it):
                kind, r0, nt = tiles[it]
                b = it % 2
                xt = XT[b]; txt = tXT[b]; sa = SSa[b]; tsa = tSSa[b]
                nsub = (nt + 127) // 128
                npm = min(128, nt)
                for s_ in range(nsub):
                    np_ = min(128, nt - s_ * 128)
                    kb.op("act", lambda e: e.activation(out=junk[0:np_, :], in_=xt[0:np_, s_, :], func=AF.Square,
                                                        accum_out=sa[0:np_, s_:s_ + 1]), reads=[txt], writes=[tjunk, tsa])
                kb.op("act", lambda e: e.activation(out=sa[0:npm, 4:4 + nsub], in_=sa[0:npm, 0:nsub], func=AF.Sqrt,
                                                    scale=1.0 / D, bias=EPS), reads=[tsa], writes=[tsa])
                kb.op("dve", lambda e: e.reciprocal(out=sa[0:npm, 0:nsub], in_=sa[0:npm, 4:4 + nsub]), reads=[tsa], writes=[tsa])
                for s_ in range(nsub):
                    np_ = min(128, nt - s_ * 128)
                    eng = "dve" if s_ % 2 == 0 else "pool"
                    kb.op(eng, lambda e: e.tensor_scalar(out=XNa[b][0:np_, s_, :], in0=xt[0:np_, s_, :], scalar1=sa[0:np_, s_:s_ + 1],
                                                         scalar2=None, op0=ALU.mult), reads=[txt, tsa], writes=[tXNa[b]])

            def A_TT(it):
                kind, r0, nt = tiles[it]
                b = it % 2
                nsub = (nt + 127) // 128
                for k in range(8):
                    pb, tpb = next_pb()
                    for s_ in range(nsub):
                        np_ = min(128, nt - s_ * 128)
                        kb.op("pe", lambda e: e.transpose(out=pb[:, s_ * 128:s_ * 128 + np_], in_=XNa[b][0:np_, s_, k * 128:(k + 1) * 128],
                                                          identity=identb[0:np_, 0:np_]), reads=[tXNa[b], tconst], writes=[tpb])
                    copy_op("act", HTa[b][:, k, 0:nt], pb[:, 0:nt], [tpb], [tHTa[b]])

            def A_p2(it, part):
                kind, r0, nt = tiles[it]
                hT = HTa[it % 2]; thT = tHTa[it % 2]
                nsub = (nt + 127) // 128
                va = VA[it % 2]; tva = tVA[it % 2]; sz = SZ[it % 2]; tsz = tSZ[it % 2]
                if kind == "smp":
                    va_o = lambda c: vTs[:, c, 0:nt]
                    sz_o = lambda c: szs[:, c, 0:nt]
                    tva = tvTs; tsz = tszs
                else:
                    va_o = lambda c: va[:, c, 0:nt]
                    sz_o = lambda c: sz[:, c, 0:nt]
                for c in (range(NCH) if part == "a" else ()):
                    pv = PF[(c % 2) * 2]; tpv = tPF[(c % 2) * 2]
                    pg = PF[(c % 2) * 2 + 1]; tpg = tPF[(c % 2) * 2 + 1]
                    for k in range(8):
                        kb.op("pe", lambda e: e.matmul(pv[:, 0:nt], lhsT=Win[:, k, c * 128:(c + 1) * 128], rhs=hT[:, k, 0:nt],
                                                       start=(k == 0), stop=(k == 7)), reads=[tWin, thT], writes=[tpv])
                    for k in range(8):
                        kb.op("pe", lambda e: e.matmul(pg[:, 0:nt], lhsT=Win[:, k, CH + c * 128:CH + (c + 1) * 128],
                                                       rhs=hT[:, k, 0:nt], start=(k == 0), stop=(k == 7)),
                              reads=[tWin, thT], writes=[tpg])
                    s_ = sg[c % 2]; ts_ = tsg[c % 2]
                    kb.op("act", lambda e: e.activation(out=s_[:, 0:nt], in_=pg[:, 0:nt], func=AF.Sigmoid),
                          reads=[tpg], writes=[ts_])
                    kb.op("dve", lambda e: e.tensor_tensor(out=va_o(c), in0=pv[:, 0:nt], in1=s_[:, 0:nt], op=ALU.mult),
                          reads=[tpv, ts_], writes=[tva], skip_self=True)
                if part == "a":
                    return
                if kind != "pre":
                    for c in range(NCH):
                        pz = PF[4 + c % 2]; tpz = tPF[4 + c % 2]
                        for k in range(8):
                            kb.op("pe", lambda e: e.matmul(pz[:, 0:nt], lhsT=Win[:, k, 2 * CH + c * 128:2 * CH + (c + 1) * 128],
                                                           rhs=hT[:, k, 0:nt], start=(k == 0), stop=(k == 7)),
                                  reads=[tWin, thT], writes=[tpz])
                        kb.op("act", lambda e: e.activation(out=sz_o(c), in_=pz[:, 0:nt], func=AF.Silu),
                              reads=[tpz], writes=[tsz], skip_self=True)
                special = (kind == "smp") or (kind == "main" and r0 + nt == RP)
                if special:
                    ntm = 16 if kind == "smp" else 32
                    t0 = 0 if kind == "smp" else nt - 32
                    for n in range(4):
                        pv = PF[0]; pg = PF[1]
                        for k in range(8):
                            kb.op("pe", lambda e: e.matmul(pv[0:ntm, :], lhsT=hT[:, k, t0:t0 + ntm],
                                                           rhs=Win[:, k, n * 512:(n + 1) * 512], start=(k == 0), stop=(k == 7)),
                                  reads=[tWin, thT], writes=[tPF[0]])
                        for k in range(8):
                            kb.op("pe", lambda e: e.matmul(pg[0:ntm, :], lhsT=hT[:, k, t0:t0 + ntm],
                                                           rhs=Win[:, k, CH + n * 512:CH + (n + 1) * 512], start=(k == 0), stop=(k == 7)),
                                  reads=[tWin, thT], writes=[tPF[1]])
                        kb.op("act", lambda e: e.activation(out=sg[0][0:ntm, :], in_=pg[0:ntm, :], func=AF.Sigmoid),
                              reads=[tPF[1]], writes=[tsg[0]])
                        kb.op("dve", lambda e: e.tensor_tensor(out=vtm[0:ntm, n * 512:(n + 1) * 512], in0=pv[0:ntm, :],
                                                               in1=sg[0][0:ntm, :], op=ALU.mult),
                              reads=[tPF[0], tsg[0]], writes=[tvtm])
                    if kind == "smp":
                        for s in range(4):
                            kb.dma("pool", "vtm", ncs.ap()[s, 26:30, :], vtm[4 * s:4 * s + 4, :], reads=[tvtm], is_out=True)
                    else:
                        kb.dma("pool", "vtm", ncp.ap(), vtm[0:32, :], reads=[tvtm], is_out=True)
                if kind == "pre":
                    kb.dma("pool", "vallA%d" % (it % 2), dap(vpre_d, 0, [[NCH * PRE, 128], [PRE, NCH], [1, PRE]]),
                           va[:, :, 0:PRE], reads=[tva], writes=[DT("vT", -1)])
                elif kind == "main":
                    ti = (r0 - PRE) // TA
                    kb.dma("pool", "vallA%d" % (it % 2), vT_d.ap()[ti], sap(va, 0, 128, 0, [[1, NCH * TA]]),
                           reads=[tva], writes=[DT("vT", ti)])
                    kb.dma("pool", "szallA%d" % (it % 2), szT_d.ap()[ti], sap(sz, 0, 128, 0, [[1, NCH * TA]]),
                           reads=[tsz], writes=[DT("szT", ti)])

            nA = len(tiles)
            A_load(0); A_load(1)
            A_norm(0); A_TT(0)
            for it in range(nA):
                if it + 1 < nA:
                    A_norm(it + 1)
                A_p2(it, "a")
                if it + 1 < nA:
                    A_TT(it + 1)
                A_p2(it, "b")
                if it + 2 < nA:
                    A_load(it + 2)
            for s in range(4):
                kb.dma("sp", "d2d", ncs.ap()[s, 0:26, :], cconv.ap()[s, 4:30, :], is_out=True)
            kb.barrier()
        if debug == "A":
            kb.finish()
            return nc

        with ExitStack() as es:
            Dg = sbt(es, "Dg", [128, NCH, CW, 128], BF16); tDg = Tok("Dg")
            tDgE = {"dve": Tok("DgD"), "pool": Tok("DgP"), "act": Tok("DgA")}
            with ExitStack() as es_p:
                prm = sbt(es_p, "prm", [34, CH], F32); tprm = Tok("prm")
                kb.dma("sp", "prm", prm[0:31, :], a_conv_w.ap()[0], writes=[tprm])
                kb.dma("sp", "prm", prm[31:32, :], a_conv_b.ap(), writes=[tprm])
                kb.dma("sp", "prm", prm[32:33, :], a_ln_g.ap(), writes=[tprm])
                kb.dma("sp", "prm", prm[33:34, :], a_ln_b.ap(), writes=[tprm])
                for c in range(NCH):
                    pf = PF[c % 2]; tpf = tPF[c % 2]
                    kb.op("pe", lambda e: e.transpose(out=pf[:, 0:34], in_=prm[0:34, c * 128:(c + 1) * 128], identity=identf[0:34, 0:34]),
                          reads=[tprm, tconst], writes=[tpf])
                    copy_op(evac_eng(), cw[:, c, :], pf[:, 0:34], [tpf, tcw], [tcw])
                for c in range(NCH):
                    eng = ("act", "dve", "act", "dve", "act", "dve", "act", "pool")[c % 8]
                    for j in range(CW):
                        if eng == "act":
                            kb.op("act", lambda e: e.activation(out=Dg[:, c, j, :], in_=identb[:], func=AF.Copy, scale=cw[:, c, j:j + 1]),
                                  reads=[tconst, tcw], writes=[tDgE[eng]], skip_self=not (j == 0 and c < 8))
                        else:
                            kb.op(eng, lambda e: e.tensor_scalar(out=Dg[:, c, j, :], in0=identb[:], scalar1=cw[:, c, j:j + 1],
                                                                 scalar2=None, op0=ALU.mult), reads=[tconst, tcw], writes=[tDgE[eng]],
                                  skip_self=not (j == 0 and c < 8))
                kb.barrier()
            TB = 256
            with ExitStack() as es_m:
                VIN = [sbt(es_m, "vin%d" % i, [128, NCH, 30 + TB], BF16) for i in range(2)]; tVIN = [Tok("vin%d" % i) for i in range(2)]
                VST = [sbt(es_m, "vst%d" % i, [128, NCH * TB], BF16) for i in range(2)]; tVST = [Tok("vst%d" % i) for i in range(2)]
                CTO = [sbt(es_m, "cto%d" % i, [128, 8, TB], F32) for i in range(2)]; tCTO = [Tok("cto%d" % i) for i in range(2)]
                for t in range(R // TB):
                    vin = VIN[t % 2]; tvin = tVIN[t % 2]; vst = VST[t % 2]; tvst = tVST[t % 2]
                    kb.dma("sp", "vst%d" % (t % 2), vst[:, :], vT_d.ap()[t], reads=[DT("vT", t)], writes=[tvst])
                    if t == 0:
                        kb.dma("sp", "vin0", vin[:, :, 0:30], dap(vpre_d, 2, [[NCH * PRE, 128], [PRE, NCH], [1, 30]]),
                               reads=[DT("vT", -1)], writes=[tvin])
                    else:
                        kb.op("dve", lambda e: e.tensor_copy(out=vin[:, :, 0:30], in_=VIN[(t - 1) % 2][:, :, TB:TB + 30]),
                              reads=[tVIN[(t - 1) % 2]], writes=[tvin])
                    kb.op("dve", lambda e: e.tensor_copy(out=vin[:, :, 30:30 + TB], in_=sap(vst, 0, 128, 0, [[TB, NCH], [1, TB]])),
                          reads=[tvst, tvin], writes=[tvin])
                    for c in range(NCH):
                        hf = c // 8
                        cto = CTO[hf]; tcto = tCTO[hf]
                        pf = PF[c % 6]; tpf = tPF[c % 6]
                        for j in range(CW):
                            kb.op("pe", lambda e: e.matmul(pf[:, 0:TB], lhsT=Dg[:, c, j, :], rhs=vin[:, c, j:j + TB],
                                                           start=(j == 0), stop=(j == CW - 1)), reads=[tDgE["dve"], tDgE["pool"], tDgE["act"], tvin], writes=[tpf])
                        kb.op("act", lambda e: e.activation(out=cto[:, c % 8, :], in_=pf[:, 0:TB], func=AF.Identity,
                                                            bias=cw[:, c, 31:32], scale=1.0), reads=[tpf, tcw, tcto], writes=[tcto])
                        if c % 8 == 7:
                            kb.dma("pool", "cto%d" % hf, dap(cT_d, t * 128 * NCH * 256 + hf * 8 * 256, [[NCH * 256, 128], [1, 8 * 256]]),
                                   sap(cto, 0, 128, 0, [[1, 8 * TB]]), reads=[tcto], writes=[DT("cT", (t, hf))])
                kb.barrier()
            prm = sbt(es, "hist", [30, CH], F32); tprm = Tok("hist")
            vins = sbt(es, "vins", [128, NCH, 4, 34], BF16); tvins = Tok("vins")
            for s in range(4):
                kb.dma("sp", "hist", prm[0:30, :], cconv.ap()[s], writes=[tprm])
                pf = PF[s % 2]; tpf = tPF[s % 2]
                for c in range(NCH):
                    kb.op("pe", lambda e: e.transpose(out=pf[:, c * 30:(c + 1) * 30], in_=prm[0:30, c * 128:(c + 1) * 128],
                                                      identity=identf[0:30, 0:30]), reads=[tprm, tconst], writes=[tpf])
                kb.op("dve", lambda e: e.tensor_copy(out=sap(vins, 0, 128, s * 34, [[4 * 34, NCH], [1, 30]]),
                                                     in_=sap(pf, 0, 128, 0, [[30, NCH], [1, 30]])), reads=[tpf, tvins], writes=[tvins])
            kb.op("dve", lambda e: e.tensor_copy(out=sap(vins, 0, 128, 30, [[4 * 34, NCH], [34, 4], [1, 4]]),
                                                 in_=sap(vTs, 0, 128, 0, [[16, NCH], [4, 4], [1, 4]])), reads=[tvTs, tvins], writes=[tvins])
            for c in range(NCH):
                pf = PF[c % 6]; tpf = tPF[c % 6]
                for j in range(CW):
                    kb.op("pe", lambda e: e.matmul(pf[:, 0:16], lhsT=Dg[:, c, j, :],
                                                   rhs=sap(vins, 0, 128, c * 4 * 34 + j, [[34, 4], [1, 4]]),
                                                   start=(j == 0), stop=(j == CW - 1)), reads=[tDgE["dve"], tDgE["pool"], tDgE["act"], tvins], writes=[tpf])
                kb.op("act", lambda e: e.activation(out=cTs[:, c, :], in_=pf[:, 0:16], func=AF.Identity,
                                                    bias=cw[:, c, 31:32], scale=1.0), reads=[tpf, tcw, tcTs], writes=[tcTs])
            kb.barrier()
        if debug == "B":
            kb.finish()
            return nc

        TC = 256
        with ExitStack() as es:
            Wout = sbt(es, "Wout", [128, NCH, D], BF16); tWout = Tok("Wout")
            load_weights([(Wout, tWout, a_w_out.ap()[0], None)])
            NS = TC // 128
            CT = [sbt(es, "ctc%d" % i, [128, NCH, TC], F32) for i in range(2)]; tCT = [Tok("ctc%d" % i) for i in range(2)]
            SZc = [sbt(es, "szc%d" % i, [128, NCH, TC], BF16) for i in range(2)]; tSZc = [Tok("szc%d" % i) for i in range(2)]
            XTc = [sbt(es, "xtc%d" % i, [128, NS, D], F32) for i in range(3)]; tXTc = [Tok("xtc%d" % i) for i in range(3)]
            sq = [sbt(es, "sqc%d" % i, [128, TC], F32) for i in range(2)]; tsq = [Tok("sqc%d" % i) for i in range(2)]
            STc = [sbt(es, "stc%d" % i, [128, 4, TC], F32) for i in range(2)]; tSTc = [Tok("stc%d" % i) for i in range(2)]
            nrm = [sbt(es, "nrm%d" % i, [128, TC], F32) for i in range(2)]; tnrm = [Tok("nrm%d" % i) for i in range(2)]
            act_ = [sbt(es, "actc%d" % i, [128, TC], BF16) for i in range(2)]; tact = [Tok("actc%d" % i) for i in range(2)]
            Uc = [sbt(es, "uc%d" % i, [128, NCH, TC], BF16) for i in range(2)]; tUc = [(Tok("ucD%d" % i), Tok("ucP%d" % i)) for i in range(2)]
            tiles = [("main", t * TC, TC) for t in range(R // TC)] + [("smp", 0, 16)]
            def C_views(it):
                kind, i0, nt = tiles[it]
                b = it % 2
                smp = kind == "smp"
                ct = CT[b]; szc = SZc[b]
                if smp:
                    return (lambda c: cTs[:, c, 0:nt]), (lambda c: szs[:, c, 0:nt]), tcTs, tszs
                return (lambda c: ct[:, c, :]), (lambda c: szc[:, c, :]), tCT[b], tSZc[b]

            def C_p1(it):
                kind, i0, nt = tiles[it]
                b = it % 2
                nsub = (nt + 127) // 128
                smp = kind == "smp"
                ct = CT[b]; szc = SZc[b]; xt = XTc[it % 3]; txt = tXTc[it % 3]
                ct_v, sz_v, tct, tszc = C_views(it)
                if smp:
                    kb.dma("sp", "xtc%d" % (it % 3), xt[0:nt, 0, :], xs.ap(), writes=[txt])
                else:
                    kb.dma("sp", "ctc%d" % b, sap(ct, 0, 128, 0, [[1, NCH * TC]]), cT_d.ap()[i0 // 256],
                           reads=[DT("cT", (i0 // 256, 0)), DT("cT", (i0 // 256, 1))], writes=[tct])
                    kb.dma("sp", "szc%d" % b, sap(szc, 0, 128, 0, [[1, NCH * TC]]), szT_d.ap()[i0 // TA],
                           reads=[DT("szT", i0 // TA)], writes=[tszc])
                    kb.dma("sp", "xtc%d" % (it % 3), xt[:, 0:nsub, :], dap(xr, (PRE + i0) * D, [[D, 128], [128 * D, nsub], [1, D]]),
                           writes=[txt])
                p_sum = PF[4 * b]; p_sq = PF[4 * b + 1]
                for c in range(NCH):
                    kb.op("pe", lambda e: e.matmul(p_sum[:, 0:nt], lhsT=onesf[:, :], rhs=ct_v(c), start=(c == 0), stop=(c == NCH - 1)),
                          reads=[tct, tconst], writes=[tPF[4 * b]])
                for c in range(NCH):
                    q_ = sq[c % 2]; tq_ = tsq[c % 2]
                    kb.op("act", lambda e: e.activation(out=q_[:, 0:nt], in_=ct_v(c), func=AF.Square), reads=[tct], writes=[tq_])
                    kb.op("pe", lambda e: e.matmul(p_sq[:, 0:nt], lhsT=onesf[:, :], rhs=q_[:, 0:nt], start=(c == 0), stop=(c == NCH - 1)),
                          reads=[tq_, tconst], writes=[tPF[4 * b + 1]])

            def C_p2b(it):
                kind, i0, nt = tiles[it]
                b = it % 2
                u = Uc[b]; tu = tUc[b]
                nsub = (nt + 127) // 128
                smp = kind == "smp"
                xt = XTc[it % 3]; txt = tXTc[it % 3]
                for s in range(nsub):
                    np_ = min(128, nt - s * 128)
                    for hf in range(2):
                        pf = PF[2 + hf]; tpf = tPF[2 + hf]
                        for c in range(NCH):
                            kb.op("pe", lambda e: e.matmul(pf[0:np_, :], lhsT=u[:, c, s * 128:s * 128 + np_],
                                                           rhs=Wout[:, c, hf * 512:(hf + 1) * 512], start=(c == 0), stop=(c == NCH - 1)),
                                  reads=[tu[0], tu[1], tWout], writes=[tpf])
                        kb.op("dve", lambda e: e.tensor_tensor(out=xt[0:np_, s, hf * 512:(hf + 1) * 512],
                                                               in0=xt[0:np_, s, hf * 512:(hf + 1) * 512], in1=pf[0:np_, :], op=ALU.add),
                              reads=[tpf, txt], writes=[txt])
                if smp:
                    copy_op("dve", x1s[0:16, 0, :], xt[0:16, 0, :], [txt], [tx1s])
                else:
                    kb.dma("pool", "xtc%d" % (it % 3), dap(x1_d, i0 * D, [[D, 128], [128 * D, nsub], [1, D]]), xt[:, 0:nsub, :],
                           reads=[txt], writes=[DT("x1", i0 // TC)])


            def C_p2(it, part):
                kind, i0, nt = tiles[it]
                b = it % 2
                u = Uc[b]; tu = tUc[b]; st = STc[b]; tst = tSTc[b]
                nsub = (nt + 127) // 128
                smp = kind == "smp"
                xt = XTc[it % 3]; txt = tXTc[it % 3]
                ct_v, sz_v, tct, tszc = C_views(it)
                p_sum = PF[4 * b]; p_sq = PF[4 * b + 1]
                if part == "b":
                    return C_p2b(it)
                mean = st[:, 0, 0:nt]; rstd = st[:, 1, 0:nt]; mr = st[:, 2, 0:nt]; tmp = st[:, 3, 0:nt]
                kb.op("dve", lambda e: e.tensor_scalar(out=mean, in0=p_sum[:, 0:nt], scalar1=1.0 / CH, scalar2=None, op0=ALU.mult),
                      reads=[tPF[4 * b], tst], writes=[tst])
                kb.op("dve", lambda e: e.tensor_tensor(out=tmp, in0=mean, in1=mean, op=ALU.mult), reads=[tst], writes=[tst])
                kb.op("dve", lambda e: e.scalar_tensor_tensor(out=tmp, in0=p_sq[:, 0:nt], scalar=1.0 / CH, in1=tmp,
                                                              op0=ALU.mult, op1=ALU.subtract), reads=[tPF[4 * b + 1], tst], writes=[tst])
                kb.op("act", lambda e: e.activation(out=tmp, in_=tmp, func=AF.Sqrt, scale=1.0, bias=EPS), reads=[tst], writes=[tst])
                kb.op("dve", lambda e: e.reciprocal(out=rstd, in_=tmp), reads=[tst], writes=[tst])
                kb.op("dve", lambda e: e.tensor_tensor(out=mr, in0=mean, in1=rstd, op=ALU.mult), reads=[tst], writes=[tst])
                for c in range(NCH):
                    n_ = nrm[c % 2]; tn_ = tnrm[c % 2]; a_ = act_[c % 2]; ta_ = tact[c % 2]
                    kb.op("dve", lambda e: e.tensor_tensor(out=n_[:, 0:nt], in0=ct_v(c), in1=rstd, op=ALU.mult),
                          reads=[tct, tst], writes=[tn_])
                    kb.op("dve", lambda e: e.tensor_tensor(out=n_[:, 0:nt], in0=n_[:, 0:nt], in1=mr, op=ALU.subtract),
                          reads=[tn_, tst], writes=[tn_])
                    kb.op("act", lambda e: e.activation(out=a_[:, 0:nt], in_=n_[:, 0:nt], func=AF.Silu, scale=cw[:, c, 32:33],
                                                        bias=cw[:, c, 33:34]), reads=[tn_, tcw], writes=[ta_])
                    kb.op("dve" if c % 4 else "pool", lambda e: e.tensor_tensor(out=u[:, c, 0:nt], in0=a_[:, 0:nt], in1=sz_v(c), op=ALU.mult),
                          reads=[ta_, tszc], writes=[tu[0 if c % 4 else 1]], skip_self=True)

            nC = len(tiles)
            C_p1(0)
            for it in range(nC):
                if it + 1 < nC:
                    C_p1(it + 1)
                C_p2(it, "a")
                if it >= 1:
                    C_p2b(it - 1)
            C_p2b(nC - 1)
            kb.barrier()
        if debug == "C":
            kb.finish()
            return nc

        with ExitStack() as es:
            Wkv = sbt(es, "Wkv", [128, 8, 3072], BF16); tWkv = Tok("Wkv")
            Wb = sbt(es, "Wb", [128, 8, 2048], BF16); tWb = Tok("Wb")
            load_weights([(Wkv, tWkv, w_kv.ap(), (kv_norm, 0)), (Wb, tWb, b_w_in.ap()[0], (b_norm, 0))])
            TD = 128
            XTd = [sbt(es, "xtd%d" % i, [128, 1, D], F32) for i in range(2)]; tXTd = [Tok("xtd%d" % i) for i in range(2)]
            xn = sbt(es, "xnd", [128, 1, D], BF16); txn = Tok("xnd")
            junk = sbt(es, "junkd", [128, D], BF16); tjunk = Tok("junkd")
            hT = sbt(es, "hTd", [128, 8, TD], BF16); thT = Tok("hTd")
            KQ = [sbt(es, "kq%d" % i, [128, 6, 512], F32) for i in range(2)]; tKQ = [Tok("kq%d" % i) for i in range(2)]
            Vt = [sbt(es, "vt%d" % i, [128, 3, 512], F32) for i in range(2)]; tVt = [Tok("vt%d" % i) for i in range(2)]
            kvb = [sbt(es, "kvb%d" % i, [128, 3, 2, 512], BF16) for i in range(2)]
            tkvbK = [Tok("kvbK%d" % i) for i in range(2)]; tkvbV = [Tok("kvbV%d" % i) for i in range(2)]
            qb = [sbt(es, "qb%d" % i, [128, 3, 512], BF16) for i in range(2)]; tqb = [Tok("qb%d" % i) for i in range(2)]
            szgb = [sbt(es, "szgb%d" % i, [128, 512], BF16) for i in range(2)]; tszgb = [Tok("szgb%d" % i) for i in range(2)]
            sqh = sbt(es, "sqh", [128, 6, 512], F32); tsqh = Tok("sqh")
            HS = [sbt(es, "hs%d" % i, [128, 96], F32) for i in range(2)]; tHS = [Tok("hs%d" % i) for i in range(2)]
            RPp = [sbt(es, "rp0", [128, 4, 48 * 8], F32)] * 2; tRP = [Tok("rp0")] * 2
            cs = [sbt(es, "cs%d" % i, [128, 16], F32) for i in range(3)]; tcs = [Tok("cs%d" % i) for i in range(3)]
            gk = sbt(es, "gk", [128, 2, 512], F32); tgk = Tok("gk")
            kb.dma("sp", "gk", sap(gk, 0, 128, 0, [[64, 8], [1, 64]]), dap(k_norm, 0, [[0, 128], [0, 8], [1, 64]]), writes=[tgk])
            kb.dma("sp", "gk", sap(gk, 0, 128, 512, [[64, 8], [1, 64]]), dap(q_norm, 0, [[0, 128], [0, 8], [1, 64]]), writes=[tgk])

            tiles = [("main", t * TD, TD) for t in range(R // TD)] + [("smp", 0, 16)]
            def D_common(it):
                kind, i0, nt = tiles[it]
                b = it % 2
                smp = kind == "smp"
                is_main = (not smp) and i0 >= HALO
                withq = smp or is_main
                ns = 6 if withq else 3
                tail = (not smp) and (i0 >= R - 2048)
                return kind, i0, nt, b, smp, is_main, withq, ns, ns * 8, tail

            def D_p1(it):
                kind, i0, nt, b, smp, is_main, withq, ns, nh, tail = D_common(it)
                np_ = nt
                if smp:
                    xt = x1s; txt = tx1s
                    kb.dma("sp", "cs%d" % b, cs[b][0:16, :], css.ap(), writes=[tcs[b]])
                else:
                    xt = XTd[b]; txt = tXTd[b]
                    kb.dma("sp", "xtd%d" % b, xt[:, 0, :], x1_d.ap()[i0:i0 + 128, :], reads=[DT("x1", i0 // TC)], writes=[txt])
                    kb.dma("sp", "cs%d" % b, cs[b][:, :], csp.ap()[i0:i0 + 128, :], writes=[tcs[b]])
                rms_T(xt, txt, nt, hT, thT, xn, txn, junk, tjunk)
                kq = KQ[b]; tkq = tKQ[b]; vt = Vt[b]; tvt = tVt[b]; hs = HS[b]; ths = tHS[b]; rp = RPp[b]; trp = tRP[b]
                kbf = kvb[b]
                tail = (not smp) and (i0 >= R - 2048)
                for n in range(6):
                    g = n % 3; isv = n // 3
                    pf = PF[n % 6]; tpf = tPF[n % 6]
                    for k in range(8):
                        kb.op("pe", lambda e: e.matmul(pf[0:np_, :], lhsT=hT[:, k, 0:np_], rhs=Wkv[:, k, n * 512:(n + 1) * 512],
                                                       start=(k == 0), stop=(k == 7)), reads=[thT, tWkv], writes=[tpf])
                    if isv:
                        if not smp:
                            kb.op("act", lambda e: e.activation(out=kbf[0:np_, g, 1, :], in_=pf[0:np_, :], func=AF.Copy),
                                  reads=[tpf], writes=[tkvbV[b]], skip_self=True)
                        if tail or smp:
                            kb.op("dve", lambda e: e.tensor_copy(out=vt[0:np_, g, :], in_=pf[0:np_, :]),
                                  reads=[tpf] + ([] if smp else [tkvbV[b]]), writes=[tvt])
                    else:
                        copy_op("act" if g != 1 else "dve", kq[0:np_, g, :], pf[0:np_, :], [tpf], [tkq])
                if withq:
                    for n in range(4):
                        pf = PF[n % 6]; tpf = tPF[n % 6]
                        for k in range(8):
                            kb.op("pe", lambda e: e.matmul(pf[0:np_, :], lhsT=hT[:, k, 0:np_], rhs=Wb[:, k, n * 512:(n + 1) * 512],
                                                           start=(k == 0), stop=(k == 7)), reads=[thT, tWb], writes=[tpf])
                        if n < 3:
                            copy_op("act" if n != 1 else "dve", kq[0:np_, 3 + n, :], pf[0:np_, :], [tpf], [tkq])
                        elif smp:
                            kb.op("act", lambda e: e.activation(out=szgs[0:np_, :], in_=pf[0:np_, :], func=AF.Silu),
                                  reads=[tpf], writes=[tszgs])
                        else:
                            kb.op("act", lambda e: e.activation(out=szgb[b][0:np_, :], in_=pf[0:np_, :], func=AF.Silu),
                                  reads=[tpf], writes=[tszgb[b]])

            def D_p2(it, part=None):
                kind, i0, nt, b, smp, is_main, withq, ns, nh, tail = D_common(it)
                np_ = nt
                kq = KQ[b]; tkq = tKQ[b]; vt = Vt[b]; tvt = tVt[b]; hs = HS[b]; ths = tHS[b]; rp = RPp[b]; trp = tRP[b]
                kbf = kvb[b]
                kqv = lambda off, n: sap(kq, 0, np_, off, [[64, nh], [1, n]])
                if part in (None, "A"):
                    kb.op("act", lambda e: e.activation(out=sqh[0:np_, 0:ns, :], in_=kq[0:np_, 0:ns, :], func=AF.Square), reads=[tkq], writes=[tsqh])
                if part == "A":
                    return
                kb.op("dve", lambda e: e.tensor_reduce(out=hs[0:np_, 0:nh], in_=sap(sqh, 0, np_, 0, [[64, nh], [1, 64]]),
                                                       axis=AX.X, op=ALU.add), reads=[tsqh], writes=[ths])
                kb.op("act", lambda e: e.activation(out=hs[0:np_, 48:48 + nh], in_=hs[0:np_, 0:nh], func=AF.Sqrt, scale=1.0 / 64, bias=EPS),
                      reads=[ths], writes=[ths])
                kb.op("dve", lambda e: e.reciprocal(out=hs[0:np_, 0:nh], in_=hs[0:np_, 48:48 + nh]), reads=[ths], writes=[ths])
                kb.op("dve", lambda e: e.tensor_tensor(out=kqv(0, 64), in0=kqv(0, 64), in1=sap(hs, 0, np_, 0, [[1, nh], [0, 64]]), op=ALU.mult),
                      reads=[tkq, ths], writes=[tkq])
                kb.op("pool", lambda e: e.tensor_tensor(out=kq[0:np_, 0:3, :], in0=kq[0:np_, 0:3, :],
                                                        in1=sap(gk, 0, np_, 0, [[0, 3], [1, 512]]), op=ALU.mult), reads=[tkq, tgk], writes=[tkq])
                if withq:
                    kb.op("pool", lambda e: e.tensor_tensor(out=kq[0:np_, 3:6, :], in0=kq[0:np_, 3:6, :],
                                                            in1=sap(gk, 0, np_, 512, [[0, 3], [1, 512]]), op=ALU.mult), reads=[tkq, tgk], writes=[tkq])
                cosv = sap(cs[it % 3], 0, np_, 0, [[0, nh], [1, 8]]); sinv = sap(cs[it % 3], 0, np_, 8, [[0, nh], [1, 8]])
                r_ = lambda i: sap(rp, 0, np_, i * 384, [[8, nh], [1, 8]])
                x1v = kqv(0, 8); x2v = kqv(8, 8)
                tcst = tcs[it % 3]
                kb.op("dve", lambda e: e.tensor_tensor(out=r_(0), in0=x1v, in1=cosv, op=ALU.mult), reads=[tkq, tcst], writes=[trp])
                kb.op("dve", lambda e: e.tensor_tensor(out=r_(1), in0=x2v, in1=sinv, op=ALU.mult), reads=[tkq, tcst], writes=[trp], skip_self=True)
                kb.op("dve", lambda e: e.tensor_tensor(out=r_(2), in0=x2v, in1=cosv, op=ALU.mult), reads=[tkq, tcst], writes=[trp], skip_self=True)
                kb.op("dve", lambda e: e.tensor_tensor(out=r_(3), in0=x1v, in1=sinv, op=ALU.mult), reads=[tkq, tcst], writes=[trp], skip_self=True)
                kb.op("dve", lambda e: e.tensor_tensor(out=x1v, in0=r_(0), in1=r_(1), op=ALU.subtract), reads=[trp, tkq], writes=[tkq])
                kb.op("dve", lambda e: e.tensor_tensor(out=x2v, in0=r_(2), in1=r_(3), op=ALU.add), reads=[trp, tkq], writes=[tkq])
                if smp:
                    copy_op("pool", kvs[0:16, :, 0, :], kq[0:16, 0:3, :], [tkq], [tkvs])
                    copy_op("pool", kvs[0:16, :, 1, :], vt[0:16, :, :], [tvt, tkvs], [tkvs])
                    copy_op("pool", qs[0:16, :, :], kq[0:16, 3:6, :], [tkq], [tqs])
                else:
                    copy_op("dve", kbf[:, :, 0, :], kq[:, 0:3, :], [tkq], [tkvbK[b]])
                    kb.dma("pool", "kvb%d" % b, kv_d.ap()[i0:i0 + 128], kbf[:, :, :, :], reads=[tkvbK[b], tkvbV[b]],
                           writes=[DT("kv", i0 // 128)])
                    for g in range(3):
                        if i0 >= R - WINS[g]:
                            o0 = i0 - (R - WINS[g])
                            kb.dma("pool", "kq%d" % b, nkp[g].ap()[o0:o0 + 128, 0, :], kq[:, g, :], reads=[tkq], is_out=True)
                            kb.dma("pool", "vt%d" % b, nkp[g].ap()[o0:o0 + 128, 1, :], vt[:, g, :], reads=[tvt], is_out=True)
                    if is_main:
                        tm = i0 - HALO
                        copy_op("dve", qb[b][:, :, :], kq[:, 3:6, :], [tkq], [tqb[b]])
                        kb.dma("pool", "qb%d" % b, q_d.ap()[tm:tm + 128], qb[b][:, :, :], reads=[tqb[b]], writes=[DT("q", tm // 128)])
                        kb.dma("pool", "szgb%d" % b, szg_d.ap()[tm:tm + 128], szgb[b][:, :], reads=[tszgb[b]], writes=[DT("szg", tm // 128)])

            XN2 = [xn, sbt(es, "xnd2", [128, 1, D], BF16)]; tXN2 = [txn, Tok("xnd2")]
            SSD = [sbt(es, "ssd%d" % i, [128, 4], F32) for i in range(2)]; tSSD = [Tok("ssd%d" % i) for i in range(2)]

            def D_xt(it):
                kind, i0, nt, b, smp, is_main, withq, ns, nh, tail = D_common(it)
                return (x1s, tx1s) if smp else (XTd[b], tXTd[b])

            def D_load(it):
                kind, i0, nt, b, smp, is_main, withq, ns, nh, tail = D_common(it)
                c3 = it % 3
                if smp:
                    kb.dma("sp", "cs%d" % c3, cs[c3][0:16, :], css.ap(), writes=[tcs[c3]])
                else:
                    kb.dma("sp", "xtd%d" % b, XTd[b][:, 0, :], x1_d.ap()[i0:i0 + 128, :], reads=[DT("x1", i0 // TC)], writes=[tXTd[b]])
                    kb.dma("sp", "cs%d" % c3, cs[c3][:, :], csp.ap()[i0:i0 + 128, :], writes=[tcs[c3]])

            def D_normA(it):
                kind, i0, nt, b, smp, is_main, withq, ns, nh, tail = D_common(it)
                xt, txt = D_xt(it)
                kb.op("act", lambda e: e.activation(out=junk[0:nt, :], in_=xt[0:nt, 0, :], func=AF.Square,
                                                    accum_out=SSD[b][0:nt, 0:1]), reads=[txt], writes=[tjunk, tSSD[b]])

            def D_normB(it):
                kind, i0, nt, b, smp, is_main, withq, ns, nh, tail = D_common(it)
                xt, txt = D_xt(it)
                sd = SSD[b]; tsd = tSSD[b]
                kb.op("act", lambda e: e.activation(out=sd[0:nt, 1:2], in_=sd[0:nt, 0:1], func=AF.Sqrt, scale=1.0 / D, bias=EPS),
                      reads=[tsd], writes=[tsd])
                kb.op("dve", lambda e: e.reciprocal(out=sd[0:nt, 2:3], in_=sd[0:nt, 1:2]), reads=[tsd], writes=[tsd])
                kb.op("dve", lambda e: e.tensor_scalar(out=XN2[b][0:nt, 0, :], in0=xt[0:nt, 0, :], scalar1=sd[0:nt, 2:3],
                                                       scalar2=None, op0=ALU.mult), reads=[txt, tsd], writes=[tXN2[b]])

            def D_TT(it):
                kind, i0, nt, b, smp, is_main, withq, ns, nh, tail = D_common(it)
                for half in range(2):
                    pb, tpb = next_pb()
                    for k4 in range(4):
                        k = half * 4 + k4
                        kb.op("pe", lambda e: e.transpose(out=pb[:, k4 * 128:k4 * 128 + nt], in_=XN2[b][0:nt, 0, k * 128:(k + 1) * 128],
                                                          identity=identb[0:nt, 0:nt]), reads=[tXN2[b], tconst], writes=[tpb])
                    kb.op("act", lambda e: e.activation(out=sap(hT, 0, 128, half * 4 * TD, [[TD, 4], [1, nt]]),
                                                        in_=sap(pb, 0, 128, 0, [[128, 4], [1, nt]]), func=AF.Copy),
                          reads=[tpb], writes=[thT])

            def D_MM(it):
                kind, i0, nt, b, smp, is_main, withq, ns, nh, tail = D_common(it)
                np_ = nt
                kq = KQ[b]; tkq = tKQ[b]; vt = Vt[b]; tvt = tVt[b]
                kbf = kvb[b]
                for n in range(6):
                    g = n % 3; isv = n // 3
                    pf = PF[n % 6]; tpf = tPF[n % 6]
                    for k in range(8):
                        kb.op("pe", lambda e: e.matmul(pf[0:np_, :], lhsT=hT[:, k, 0:np_], rhs=Wkv[:, k, n * 512:(n + 1) * 512],
                                                       start=(k == 0), stop=(k == 7)), reads=[thT, tWkv], writes=[tpf])
                    if isv:
                        if not smp:
                            kb.op("act", lambda e: e.activation(out=kbf[0:np_, g, 1, :], in_=pf[0:np_, :], func=AF.Copy),
                                  reads=[tpf], writes=[tkvbV[b]], skip_self=True)
                        if tail or smp:
                            kb.op("act", lambda e: e.activation(out=vt[0:np_, g, :], in_=pf[0:np_, :], func=AF.Copy),
                                  reads=[tpf], writes=[tvt], skip_self=True)
                    else:
                        kb.op("act", lambda e: e.activation(out=kq[0:np_, g, :], in_=pf[0:np_, :], func=AF.Copy),
                              reads=[tpf], writes=[tkq], skip_self=(g > 0))
                if withq:
                    for n in range(4):
                        pf = PF[n % 6]; tpf = tPF[n % 6]
                        for k in range(8):
                            kb.op("pe", lambda e: e.matmul(pf[0:np_, :], lhsT=hT[:, k, 0:np_], rhs=Wb[:, k, n * 512:(n + 1) * 512],
                                                           start=(k == 0), stop=(k == 7)), reads=[thT, tWb], writes=[tpf])
                        if n < 3:
                            kb.op("act", lambda e: e.activation(out=kq[0:np_, 3 + n, :], in_=pf[0:np_, :], func=AF.Copy),
                                  reads=[tpf], writes=[tkq], skip_self=True)
                        elif smp:
                            kb.op("act", lambda e: e.activation(out=szgs[0:np_, :], in_=pf[0:np_, :], func=AF.Silu),
                                  reads=[tpf], writes=[tszgs])
                        else:
                            kb.op("act", lambda e: e.activation(out=szgb[b][0:np_, :], in_=pf[0:np_, :], func=AF.Silu),
                                  reads=[tpf], writes=[tszgb[b]])

            nD = len(tiles)
            D_load(0); D_load(1)
            D_normA(0); D_normB(0)
            for it in range(nD):
                if it + 1 < nD:
                    D_normA(it + 1)
                if it >= 1:
                    D_p2(it - 1, "A")
                if it + 1 < nD:
                    D_normB(it + 1)
                if it >= 1:
                    D_p2(it - 1, "B")
                D_TT(it)
                D_MM(it)
                if it + 2 < nD:
                    D_load(it + 2)
            D_p2(nD - 1, "A")
            D_p2(nD - 1, "B")
            for g in range(3):
                W = WINS[g]
                for s in range(4):
                    for r0 in range(0, W - 4, 508):
                        nr = min(508, W - 4 - r0)
                        kb.dma("sp", "d2d", nks[g].ap()[s, r0:r0 + nr],
                               dap(ck[g], (s * W + 4 + r0) * 1024, [[1024, nr], [512, 2], [1, 512]]), is_out=True)
                    kb.dma("pool", "kvs", nks[g].ap()[s, W - 4:W], kvs[4 * s:4 * s + 4, g, :, :], reads=[tkvs], is_out=True)
            kb.barrier()
        if debug == "D":
            kb.finish()
            return nc

        with ExitStack() as es:
            WoB = sbt(top, "WoB", [64, 8, D], BF16); tWoB = Tok("WoB")
            load_weights([(WoB, tWoB, b_w_out.ap()[0], None)])
            mk = sbt(es, "mk", [128, 2, 512], BF16); tmk = Tok("mk")
            mkf = sbt(es, "mkf", [128, 2, 256], F32)
            kb.dma("sp", "mk", mkf[:, 0, :], maskn_d.ap(), writes=[tmk])
            kb.dma("sp", "mk", mkf[:, 1, :], maskh_d.ap(), writes=[tmk])
            copy_op("dve", mk[:, :, 0:256], mkf[:, :, :], [tmk], [tmk])
            copy_op("dve", mk[:, :, 256:512], mkf[:, :, :], [tmk], [tmk])
            ST = 2048
            accO = sbt(es, "accO", [64, 4, ST], F32); taccO = Tok("accO")
            accD = sbt(es, "accD", [64, 4, ST], F32); taccD = Tok("accD")
            oall = sbt(es, "oall", [64, 8, ST], BF16); toall = Tok("oall")
            NB = 4
            QR = [sbt(es, "qr%d" % i, [128, 256], BF16) for i in range(NB)]; tQR = [Tok("qr%d" % i) for i in range(NB)]
            KC = [sbt(es, "kc%d" % i, [128, 2, 256], BF16) for i in range(NB)]; tKC = [Tok("kc%d" % i) for i in range(NB)]
            KP = [sbt(es, "kp%d" % i, [128, 2, 256], BF16) for i in range(NB)]; tKP = [Tok("kp%d" % i) for i in range(NB)]
            QKT = [sbt(es, "qkT%d" % i, [64, 12, 128], BF16) for i in range(2)]
            tQa = [Tok("qkTa%d" % i) for i in range(2)]; tQb = [Tok("qkTb%d" % i) for i in range(2)]
            EE = [sbt(es, "ee%d" % i, [128, 512], BF16) for i in range(2)]; tEE = [Tok("ee%d" % i) for i in range(2)]
            EM = [sbt(es, "em%d" % i, [128, 512], BF16) for i in range(2)]; tEM = [Tok("em%d" % i) for i in range(2)]
            SG = [sbt(es, "sge%d" % i, [128, 512], BF16) for i in range(2)]; tSG = [Tok("sge%d" % i) for i in range(2)]
            sgT = sbt(es, "sgT", [64, 8, 128], F32); tsgT = Tok("sgT")
            UG = [sbt(es, "ug%d" % i, [64, 8, 128], BF16) for i in range(2)]; tUG = [Tok("ug%d" % i) for i in range(2)]
            X1 = [sbt(es, "x1e%d" % i, [128, D], F32) for i in range(2)]; tX1 = [Tok("x1e%d" % i) for i in range(2)]
            cnt["blk"] = 0; cnt["unit"] = 0

            def trange(name, lo, hi):
                return [DT(name, i) for i in range(lo // 128, hi // 128 + 1)]

            for sti in range(MAIN // ST):
                T0 = HALO + sti * ST
                for hq in range(2):
                    kb.op("pool", lambda e: e.memset(accO[:, :, :], 0.0), reads=[taccO], writes=[taccO])
                    kb.op("pool", lambda e: e.memset(accD[:, :, :], 0.0), reads=[taccD], writes=[taccD])
                    pendE = [None]
                    for g in range(3):
                        d = DILS[g]
                        span = 128 * d
                        for blk in range(16):
                            base = T0 + (blk // d) * span + (blk % d)
                            off = base - T0
                            pbase = base - span
                            mki = 1 if pbase < HALO else 0
                            bn = cnt["blk"]; cnt["blk"] += 1
                            b3 = bn % NB; b = bn % 2
                            qr = QR[b3]; kc = KC[b3]; kp = KP[b3]
                            kb.dma("sp", "qr%d" % b3, qr[:, :], dap(q_d, ((base - HALO) * 3 + g) * 512 + hq * 256, [[3 * 512 * d, 128], [1, 256]]),
                                   reads=trange("q", base - HALO, base - HALO + 127 * d), writes=[tQR[b3]])
                            kb.dma("sp", "kc%d" % b3, kc[:, :, :], dap(kv_d, (base * 3 + g) * 1024 + hq * 256, [[3072 * d, 128], [512, 2], [1, 256]]),
                                   reads=trange("kv", base, base + 127 * d), writes=[tKC[b3]])
                            kb.dma("sp", "kp%d" % b3, kp[:, :, :], dap(kv_d, (pbase * 3 + g) * 1024 + hq * 256, [[3072 * d, 128], [512, 2], [1, 256]]),
                                   reads=trange("kv", pbase, pbase + 127 * d), writes=[tKP[b3]])
                            pbA, tpbA = next_pb()
                            for h4 in range(4):
                                kb.op("pe", lambda e: e.transpose(out=pbA[0:64, h4 * 128:(h4 + 1) * 128], in_=qr[:, h4 * 64:(h4 + 1) * 64],
                                                                  identity=identb[:, :]), reads=[tQR[b3], tconst], writes=[tpbA])
                            for h4 in range(4):
                                kb.op("pe", lambda e: e.transpose(out=pbA[0:64, (4 + h4) * 128:(5 + h4) * 128], in_=kc[:, 0, h4 * 64:(h4 + 1) * 64],
                                                                  identity=identb[:, :]), reads=[tKC[b3], tconst], writes=[tpbA])
                            copy_op("dve", QKT[b][:, 0:8, :], sap(pbA, 0, 64, 0, [[128, 8], [1, 128]]), [tpbA], [tQa[b]])
                            pbB, tpbB = next_pb()
                            for h4 in range(4):
                                kb.op("pe", lambda e: e.transpose(out=pbB[0:64, h4 * 128:(h4 + 1) * 128], in_=kp[:, 0, h4 * 64:(h4 + 1) * 64],
                                                                  identity=identb[:, :]), reads=[tKP[b3], tconst], writes=[tpbB])
                            copy_op("act", QKT[b][:, 8:12, :], sap(pbB, 0, 64, 0, [[128, 4], [1, 128]]), [tpbB], [tQb[b]])
                            po = PF[2 + b]; tpo = tPF[2 + b]
                            pd = PF[4 + b]; tpd = tPF[4 + b]
                            for u2 in range(2):
                                un = cnt["unit"]; cnt["unit"] += 1
                                ub = un % 2
                                psS = PF[ub]; tpsS = tPF[ub]
                                ee = EE[ub]; tee = tEE[ub]; em = EM[ub]; tem = tEM[ub]
                                for p in range(2):
                                    hh = 2 * u2 + p
                                    kb.op("pe", lambda e: e.matmul(psS[:, p * 256:p * 256 + 128], lhsT=QKT[b][:, 8 + hh, :],
                                                                   rhs=QKT[b][:, hh, :], start=True, stop=True),
                                          reads=[tQb[b], tQa[b]], writes=[tpsS])
                                    kb.op("pe", lambda e: e.matmul(psS[:, p * 256 + 128:p * 256 + 256], lhsT=QKT[b][:, 4 + hh, :],
                                                                   rhs=QKT[b][:, hh, :], start=True, stop=True),
                                          reads=[tQa[b]], writes=[tpsS])
                                kb.op("act", lambda e: e.activation(out=ee[:, :], in_=psS[:, :], func=AF.Exp, scale=SCALE),
                                      reads=[tpsS], writes=[tee])
                                kb.op("dve",
                                      lambda e: e.tensor_tensor(out=em[:, :], in0=ee[:, :], in1=mk[:, mki, :], op=ALU.mult),
                                      reads=[tee, tmk], writes=[tem])

                                def pv_phase(u2=u2, em=em, tem=tem, kp=kp, kc=kc, tkp=tKP[b3], tkc=tKC[b3], po=po, tpo=tpo, pd=pd, tpd=tpd,
                                             off=off, d=d):
                                    for p in range(2):
                                        hh = 2 * u2 + p
                                        c0 = p * 256
                                        kb.op("pe", lambda e: e.matmul(po[0:64, hh * 128:(hh + 1) * 128], lhsT=kp[:, 1, hh * 64:(hh + 1) * 64],
                                                                       rhs=em[:, c0:c0 + 128], start=True, stop=False), reads=[tkp, tem], writes=[tpo])
                                        kb.op("pe", lambda e: e.matmul(po[0:64, hh * 128:(hh + 1) * 128], lhsT=kc[:, 1, hh * 64:(hh + 1) * 64],
                                                                       rhs=em[:, c0 + 128:c0 + 256], start=False, stop=True), reads=[tkc, tem], writes=[tpo])
                                        kb.op("pe", lambda e: e.matmul(pd[0:64, hh * 128:(hh + 1) * 128], lhsT=onesb[:, :],
                                                                       rhs=em[:, c0:c0 + 128], start=True, stop=False), reads=[tconst, tem], writes=[tpd])
                                        kb.op("pe", lambda e: e.matmul(pd[0:64, hh * 128:(hh + 1) * 128], lhsT=onesb[:, :],
                                                                       rhs=em[:, c0 + 128:c0 + 256], start=False, stop=True), reads=[tconst, tem], writes=[tpd])
                                    if u2 == 1:
                                        av = lambda t_: sap(t_, 0, 64, off, [[ST, 4], [d, 128]])
                                        kb.op("dve", lambda e: e.tensor_tensor(out=av(accO), in0=av(accO), in1=sap(po, 0, 64, 0, [[128, 4], [1, 128]]),
                                                                               op=ALU.add), reads=[tpo, taccO], writes=[taccO])
                                        kb.op("dve", lambda e: e.tensor_tensor(out=av(accD), in0=av(accD), in1=sap(pd, 0, 64, 0, [[128, 4], [1, 128]]),
                                                                               op=ALU.add), reads=[tpd, taccD], writes=[taccD])

                                if KE_PIPE:
                                    if pendE[0] is not None:
                                        pendE[0]()
                                    pendE[0] = pv_phase
                                else:
                                    pv_phase()
                    if pendE[0] is not None:
                        pendE[0]()
                    pendE[0] = None
                    kb.op("dve", lambda e: e.reciprocal(out=accD[:, :, :], in_=accD[:, :, :]), reads=[taccD], writes=[taccD])
                    kb.op("dve", lambda e: e.tensor_tensor(out=oall[:, hq * 4:(hq + 1) * 4, :], in0=accO[:, :, :], in1=accD[:, :, :], op=ALU.mult),
                          reads=[taccD, taccO, toall], writes=[toall])

                def O_p1(sub):
                    tm = sti * ST + sub * 128
                    b = sub % 2
                    kb.dma("sp", "sge%d" % b, SG[b][:, :], szg_d.ap()[tm:tm + 128], reads=[DT("szg", tm // 128)], writes=[tSG[b]])
                    kb.dma("sp", "x1e%d" % b, X1[b][:, :], x1_d.ap()[HALO + tm:HALO + tm + 128], reads=[DT("x1", (HALO + tm) // TC)],
                           writes=[tX1[b]])
                    pb, tpb = next_pb()
                    for h in range(8):
                        kb.op("pe", lambda e: e.transpose(out=pb[0:64, h * 128:(h + 1) * 128], in_=SG[b][:, h * 64:(h + 1) * 64],
                                                          identity=identb[:, :]), reads=[tSG[b], tconst], writes=[tpb])
                    copy_op("act", sgT[:, :, :], sap(pb, 0, 64, 0, [[128, 8], [1, 128]]), [tpb], [tsgT])
                    kb.op("dve", lambda e: e.tensor_tensor(out=UG[b][:, :, :], in0=oall[:, :, sub * 128:(sub + 1) * 128], in1=sgT[:, :, :],
                                                           op=ALU.mult), reads=[toall, tsgT], writes=[tUG[b]])

                def O_p2(sub):
                    tm = sti * ST + sub * 128
                    b = sub % 2
                    for hf in range(2):
                        pf = PF[hf]; tpf = tPF[hf]
                        for h in range(8):
                            kb.op("pe", lambda e: e.matmul(pf[:, :], lhsT=UG[b][:, h, :], rhs=WoB[:, h, hf * 512:(hf + 1) * 512],
                                                           start=(h == 0), stop=(h == 7)), reads=[tUG[b], tWoB], writes=[tpf])
                        kb.op("dve", lambda e: e.tensor_tensor(out=X1[b][:, hf * 512:(hf + 1) * 512], in0=X1[b][:, hf * 512:(hf + 1) * 512],
                                                               in1=pf[:, :], op=ALU.add), reads=[tpf, tX1[b]], writes=[tX1[b]])
                    kb.dma("pool", "x1e%d" % b, yp.ap()[tm:tm + 128], X1[b][:, :], reads=[tX1[b]], is_out=True)

                for sub in range(ST // 128):
                    O_p1(sub)
                    O_p2(sub)
            kb.barrier()
        if debug == "E":
            kb.finish()
            return nc

        with ExitStack() as es:
            smk = sbt(es, "smk", [128, 4], F32); nmk = sbt(es, "nmk", [16, 48], F32)
            selr = sbt(es, "selr", [16, 16, 128], F32); selc = sbt(es, "selc", [128, 16, 16], F32)
            tsc = Tok("sconst")
            kb.dma("sp", "sconst", smk[:, :], smask_d.ap(), writes=[tsc])
            kb.dma("sp", "sconst", nmk[:, :], nmask_d.ap(), writes=[tsc])
            kb.dma("sp", "sconst", selr[:, :, :], selr_d.ap().rearrange("p (a b) -> p a b", a=16), writes=[tsc])
            kb.dma("sp", "sconst", selc[:, :, :], selc_d.ap().rearrange("p (a b) -> p a b", a=16), writes=[tsc])
            KVC = [sbt(es, "kvc%d" % i, [128, 2, 512], F32) for i in range(2)]; tKVC = [Tok("kvc%d" % i) for i in range(2)]
            prod = sbt(es, "prod", [128, 512], F32); tprod = Tok("prod")
            ev = sbt(es, "ev", [128, 520], F32); tev = Tok("ev")
            prodn = sbt(es, "prodn", [16, 512], F32); tprodn = Tok("prodn")
            evn = sbt(es, "evn", [16, 520], F32); tevn = Tok("evn")
            sS = sbt(es, "sS", [128, 16], F32); tsS = Tok("sS")
            sN = sbt(es, "sN", [16, 16], F32); tsN = Tok("sN")
            accs = sbt(es, "accs", [16, 520], F32); taccs = Tok("accs")
            kb.op("dve", lambda e: e.memset(accs[:, :], 0.0), writes=[taccs])
            li = 0
            for g in range(3):
                d = DILS[g]; W = WINS[g]
                for s in range(4):
                    for t in range(4):
                        row = 4 * s + t
                        if g == 0:
                            if t == 0:
                                li += 1
                                kb.dma("sp", "kvc%d" % (li % 2), KVC[li % 2][:, :, :],
                                       dap(ck[0], s * W * 1024, [[1024, 128], [512, 2], [1, 512]]), writes=[tKVC[li % 2]])
                        else:
                            li += 1
                            kb.dma("sp", "kvc%d" % (li % 2), KVC[li % 2][:, :, :],
                                   dap(ck[g], (s * W + t) * 1024, [[1024 * d, 128], [512, 2], [1, 512]]), writes=[tKVC[li % 2]])
                        kvc = KVC[li % 2]; tkvc = tKVC[li % 2]
                        pq = PF[0]; tpq = tPF[0]
                        kb.op("pe", lambda e: e.matmul(pq[:, :], lhsT=selr[:, row, :], rhs=qs[0:16, g, :], start=True, stop=True),
                              reads=[tsc, tqs], writes=[tpq])
                        kb.op("dve", lambda e: e.tensor_tensor(out=prod[:, :], in0=kvc[:, 0, :], in1=pq[:, :], op=ALU.mult),
                              reads=[tkvc, tpq, tprod], writes=[tprod])
                        kb.op("dve", lambda e: e.tensor_reduce(out=sS[:, 0:8], in_=sap(prod, 0, 128, 0, [[64, 8], [1, 64]]), axis=AX.X, op=ALU.add),
                              reads=[tprod, tsS], writes=[tsS])
                        kb.op("act", lambda e: e.activation(out=ev[:, 512:520], in_=sS[:, 0:8], func=AF.Exp, scale=SCALE),
                              reads=[tsS, tev], writes=[tev])
                        if g == 0:
                            kb.op("dve", lambda e: e.tensor_scalar(out=ev[:, 512:520], in0=ev[:, 512:520], scalar1=smk[:, t:t + 1],
                                                                   scalar2=None, op0=ALU.mult), reads=[tev, tsc], writes=[tev])
                        kb.op("dve", lambda e: e.tensor_tensor(out=sap(ev, 0, 128, 0, [[64, 8], [1, 64]]), in0=sap(kvc, 0, 128, 512, [[64, 8], [1, 64]]),
                                                               in1=sap(ev, 0, 128, 512, [[1, 8], [0, 64]]), op=ALU.mult),
                              reads=[tkvc, tev], writes=[tev])
                        kb.op("dve", lambda e: e.tensor_tensor(out=prodn[:, :], in0=kvs[0:16, g, 0, :], in1=pq[0:16, :], op=ALU.mult),
                              reads=[tkvs, tpq, tprodn], writes=[tprodn])
                        kb.op("dve", lambda e: e.tensor_reduce(out=sN[:, 0:8], in_=sap(prodn, 0, 16, 0, [[64, 8], [1, 64]]), axis=AX.X, op=ALU.add),
                              reads=[tprodn, tsN], writes=[tsN])
                        kb.op("act", lambda e: e.activation(out=evn[:, 512:520], in_=sN[:, 0:8], func=AF.Exp, scale=SCALE),
                              reads=[tsN, tevn], writes=[tevn])
                        mcol = (0 if g == 0 else 16) + row
                        kb.op("dve", lambda e: e.tensor_scalar(out=evn[:, 512:520], in0=evn[:, 512:520], scalar1=nmk[:, mcol:mcol + 1],
                                                               scalar2=None, op0=ALU.mult), reads=[tevn, tsc], writes=[tevn])
                        kb.op("dve", lambda e: e.tensor_tensor(out=sap(evn, 0, 16, 0, [[64, 8], [1, 64]]), in0=sap(kvs, 0, 16, (g * 2 + 1) * 512, [[64, 8], [1, 64]]),
                                                               in1=sap(evn, 0, 16, 512, [[1, 8], [0, 64]]), op=ALU.mult),
                              reads=[tkvs, tevn], writes=[tevn])
                        po = PF[2]; tpo = tPF[2]; pd = PF[3]; tpd = tPF[3]
                        kb.op("pe", lambda e: e.matmul(po[0:16, :], lhsT=selc[:, row, :], rhs=ev[:, 0:512], start=True, stop=False),
                              reads=[tsc, tev], writes=[tpo])
                        kb.op("pe", lambda e: e.matmul(po[0:16, :], lhsT=selc[0:16, row, :], rhs=evn[:, 0:512], start=False, stop=True),
                              reads=[tsc, tevn], writes=[tpo])
                        kb.op("pe", lambda e: e.matmul(pd[0:16, 0:8], lhsT=selc[:, row, :], rhs=ev[:, 512:520], start=True, stop=False),
                              reads=[tsc, tev], writes=[tpd])
                        kb.op("pe", lambda e: e.matmul(pd[0:16, 0:8], lhsT=selc[0:16, row, :], rhs=evn[:, 512:520], start=False, stop=True),
                              reads=[tsc, tevn], writes=[tpd])
                        kb.op("dve", lambda e: e.tensor_tensor(out=accs[:, 0:512], in0=accs[:, 0:512], in1=po[0:16, :], op=ALU.add),
                              reads=[tpo, taccs], writes=[taccs])
                        kb.op("dve", lambda e: e.tensor_tensor(out=accs[:, 512:520], in0=accs[:, 512:520], in1=pd[0:16, 0:8], op=ALU.add),
                              reads=[tpd, taccs], writes=[taccs])
            kb.op("dve", lambda e: e.reciprocal(out=accs[:, 512:520], in_=accs[:, 512:520]), reads=[taccs], writes=[taccs])
            kb.op("dve", lambda e: e.tensor_tensor(out=sap(accs, 0, 16, 0, [[64, 8], [1, 64]]), in0=sap(accs, 0, 16, 0, [[64, 8], [1, 64]]),
                                                   in1=sap(accs, 0, 16, 512, [[1, 8], [0, 64]]), op=ALU.mult), reads=[taccs], writes=[taccs])
            ugs = sbt(es, "ugs", [16, 512], BF16); tugs = Tok("ugs")
            kb.op("dve", lambda e: e.tensor_tensor(out=ugs[:, :], in0=accs[:, 0:512], in1=szgs[0:16, :], op=ALU.mult),
                  reads=[taccs, tszgs], writes=[tugs])
            pb, tpb = next_pb()
            for h in range(8):
                kb.op("pe", lambda e: e.transpose(out=pb[0:64, h * 16:(h + 1) * 16], in_=ugs[0:16, h * 64:(h + 1) * 64],
                                                  identity=identb[0:16, 0:16]), reads=[tugs, tconst], writes=[tpb])
            ugsT = sbt(es, "ugsT", [64, 8, 16], BF16); tugsT = Tok("ugsT")
            copy_op("dve", ugsT[:, :, :], sap(pb, 0, 64, 0, [[16, 8], [1, 16]]), [tpb], [tugsT])
            for hf in range(2):
                pf = PF[hf]; tpf = tPF[hf]
                for h in range(8):
                    kb.op("pe", lambda e: e.matmul(pf[0:16, :], lhsT=ugsT[:, h, :], rhs=WoB[:, h, hf * 512:(hf + 1) * 512],
                                                   start=(h == 0), stop=(h == 7)), reads=[tugsT, tWoB], writes=[tpf])
                kb.op("dve", lambda e: e.tensor_tensor(out=x1s[0:16, 0, hf * 512:(hf + 1) * 512], in0=x1s[0:16, 0, hf * 512:(hf + 1) * 512],
                                                       in1=pf[0:16, :], op=ALU.add), reads=[tpf, tx1s], writes=[tx1s])
            kb.dma("pool", "x1s", ys.ap(), x1s[0:16, 0, :], reads=[tx1s], is_out=True)
            kb.barrier()
        kb.finish()
    return nc


_NC_CACHE = {}


def _rope_table(pos):
    half = 8
    inv_freq = np.exp(-math.log(500000.0) * np.arange(half, dtype=np.float32) * np.float32(2.0 / 16)).astype(np.float32)
    ang = pos.astype(np.float32)[:, None] * inv_freq[None, :]
    return np.concatenate([np.cos(ang), np.sin(ang)], axis=1).astype(np.float32)


def make_in_maps(inputs):
    f = lambda a: np.ascontiguousarray(np.asarray(a, dtype=np.float32))
    x_prompt = f(inputs["x_prompt"]); x_sample = f(inputs["x_sample"])
    shared = {k: f(inputs[k]) for k in ("a_norm", "a_w_in", "a_conv_w", "a_conv_b", "a_ln_g", "a_ln_b", "a_w_out", "kv_norm",
                                        "w_kv", "k_norm", "b_norm", "b_w_in", "q_norm", "b_w_out")}
    kj = np.arange(128)[:, None]; qi = np.arange(128)[None, :]
    maskp = (kj >= qi).astype(np.float32); maskc = (kj <= qi).astype(np.float32)
    maskn = np.concatenate([maskp, maskc], axis=1)
    smask = (np.arange(128)[:, None] >= np.arange(4)[None, :]).astype(np.float32)
    nm = np.zeros((16, 48), np.float32)
    for s in range(4):
        for t in range(4):
            for t2 in range(4):
                nm[4 * s + t2, 4 * s + t] = 1.0 if t2 <= t else 0.0
                nm[4 * s + t2, 16 + 4 * s + t] = 1.0 if t2 == t else 0.0
    selr = np.zeros((16, 16, 128), np.float32)
    selc = np.zeros((128, 16, 16), np.float32)
    for r in range(16):
        selr[r, r, :] = 1.0
        selc[:, r, r] = 1.0
    shared.update(identf=np.eye(128, dtype=np.float32), maskn=maskn, smask=smask, nmask=nm,
                  selr=selr.reshape(16, -1), selc=selc.reshape(128, -1),
                  css=_rope_table(PAST + np.arange(16) % 4))
    in_maps = []
    for c in range(8):
        b, h = c // 2, c % 2
        start = h * MAIN
        xr = np.zeros((RP, D), np.float32)
        lo = start - HALO - PRE
        if lo >= 0:
            xr[:] = x_prompt[b, lo:start + MAIN]
        else:
            xr[-lo:] = x_prompt[b, 0:start + MAIN]
        m = dict(shared)
        m["xr"] = xr
        m["xs"] = np.ascontiguousarray(x_sample[4 * c:4 * c + 4].reshape(16, D))
        m["cconv"] = f(inputs["cache_conv"][0, 4 * c:4 * c + 4])
        for g, nm_ in enumerate(("cache_kv_w128", "cache_kv_w512", "cache_kv_w2048")):
            m["ck%d" % g] = f(inputs[nm_][4 * c:4 * c + 4])
        m["csp"] = _rope_table(start - HALO + np.arange(R))
        mh = maskn.copy()
        if h == 0:
            mh[:, 0:128] = 0.0
        m["maskh"] = mh
        in_maps.append(m)
    return in_maps


def kernel(**inputs):
    if "nc" not in _NC_CACHE:
        _NC_CACHE["nc"] = build()
    nc = _NC_CACHE["nc"]
    in_maps = make_in_maps(inputs)
    res = run_bass_kernel_spmd(nc, in_maps, core_ids=list(range(8)))
    rs = res.results
    y_prompt = np.stack([np.concatenate([rs[2 * b]["yp"], rs[2 * b + 1]["yp"]], axis=0) for b in range(4)], axis=0)
    y_sample = np.concatenate([rs[c]["ys"].reshape(4, 4, D) for c in range(8)], axis=0)
    ncp = np.stack([rs[2 * b + 1]["ncp"][2:32] for b in range(4)], axis=0)[None]
    ncs = np.concatenate([rs[c]["ncs"] for c in range(8)], axis=0)[None]
    outs = [y_prompt, y_sample, ncp, ncs]
    for g in range(3):
        W = WINS[g]
        outs.append(np.stack([rs[2 * b + 1]["nk%dp" % g].reshape(W, 2, 8, 64) for b in range(4)], axis=0))
        outs.append(np.concatenate([rs[c]["nk%ds" % g].reshape(4, W, 2, 8, 64) for c in range(8)], axis=0))
    return tuple(np.ascontiguousarray(o.astype(np.float32)) for o in outs)
```
